# Optimizing a Trainium2 kernel written in Bass

```python
import jax, jax.numpy as jnp
from jax import lax
import numpy as np

D_MODEL = 4096
BATCH = 1
SEQ = 8192
DEPTH = 1

D_A = D_MODEL // 2
CHUNK = 128
HD_A = 128
N_HEADS_A = D_A // HD_A
D_B = D_MODEL // 2
POOL_WINDOWS = (2, 4, 8, 16)
N_POOL_GROUPS = len(POOL_WINDOWS)
POOL_GD = D_B // N_POOL_GROUPS
N_BRANCHES = 2
D_FF = 4 * D_MODEL
N_MOD = 6
IN_COLS = 2 * D_A + D_B + N_BRANCHES * D_MODEL
EPS = 1e-6

kernel_name = "gated_gmlp_pool_hybrid_block"


def rmsnorm(x, g):
    xf = x.astype(jnp.float32)
    y = xf * lax.rsqrt(jnp.mean(xf * xf, axis=-1, keepdims=True) + EPS)
    return (y * g.astype(jnp.float32)).astype(x.dtype)


def layernorm(x, g, b):
    xf = x.astype(jnp.float32)
    mu = jnp.mean(xf, axis=-1, keepdims=True)
    var = jnp.mean(jnp.square(xf - mu), axis=-1, keepdims=True)
    y = (xf - mu) * lax.rsqrt(var + EPS)
    return (y * g.astype(jnp.float32) + b.astype(jnp.float32)).astype(x.dtype)


def modulate(h, shift, scale):
    return h * (1 + scale[:, None, :]) + shift[:, None, :]


def spatial_gating(u, v, ln_v_g, ln_v_b, w_spatial, b_spatial):
    B, S, _ = u.shape
    u = jax.nn.gelu(u)
    v = layernorm(jax.nn.gelu(v), ln_v_g, ln_v_b)
    n_chunks = S // CHUNK
    vb = v.reshape(B, n_chunks, CHUNK, N_HEADS_A, HD_A)
    mask = jnp.tril(jnp.ones((CHUNK, CHUNK), dtype=bool))
    w = jnp.where(mask[None], w_spatial, jnp.zeros_like(w_spatial))
    mixed = jnp.einsum('hts,bnshd->bnthd', w, vb)
    mixed = mixed + jnp.transpose(b_spatial)[None, None, :, :, None]
    return u * mixed.reshape(B, S, D_A)


def pool_mix(p, w_pool, b_pool, pool_scale):
    B, S, _ = p.shape
    pf = p.astype(jnp.float32).reshape(B, S, N_POOL_GROUPS, POOL_GD)
    cs = jnp.cumsum(pf, axis=1)
    win = jnp.array(POOL_WINDOWS, dtype=jnp.int32)
    t = jnp.arange(S, dtype=jnp.int32)[:, None]
    lo = t - win[None, :]
    gathered = cs[:, jnp.clip(lo, 0, None), jnp.arange(N_POOL_GROUPS)[None, :], :]
    lower = jnp.where((lo >= 0)[None, :, :, None], gathered, 0.0)
    count = jnp.minimum(t + 1, win[None, :]).astype(jnp.float32)
    mean = (cs - lower) / count[None, :, :, None]
    pooled = (mean - pf).astype(p.dtype)
    y = jnp.einsum('bsgc,gcd->bsgd', pooled, w_pool) + b_pool
    return y.reshape(B, S, D_B) * pool_scale


def token_mixer(h, w_in, ln_v_g, ln_v_b, w_spatial, b_spatial, w_pool, b_pool,
                pool_scale, b_gate, w_up_a, w_up_b, w_out):
    proj = jnp.einsum('bsd,de->bse', h, w_in)
    u, v, p, ga, gb = jnp.split(
        proj, [D_A, 2 * D_A, 2 * D_A + D_B, 2 * D_A + D_B + D_MODEL], axis=-1)
    y_a = spatial_gating(u, v, ln_v_g, ln_v_b, w_spatial, b_spatial)
    y_b = pool_mix(p, w_pool, b_pool, pool_scale)
    g_a = jax.nn.sigmoid(ga + b_gate[0])
    g_b = jax.nn.sigmoid(gb + b_gate[1])
    merged = (g_a * jnp.einsum('bsc,cd->bsd', y_a, w_up_a)
              + g_b * jnp.einsum('bsc,cd->bsd', y_b, w_up_b))
    return jnp.einsum('bsd,de->bse', merged, w_out)


def channel_mixer(h, w_ff1, w_ff2):
    a = jnp.einsum('bsd,df->bsf', h, w_ff1)
    return jnp.einsum('bsf,fd->bsd', jnp.square(jax.nn.relu(a)), w_ff2)


def setup_inputs(seed: int = 0) -> dict:
    key = jax.random.key(seed)
    ks = jax.random.split(key, 24)
    f32 = jnp.float32
    L = DEPTH

    def nrm(k, shape, scale):
        return jax.random.normal(k, shape, f32) * scale

    return {
        "x": nrm(ks[0], (BATCH, SEQ, D_MODEL), 1.0),
        "c": nrm(ks[1], (BATCH, D_MODEL), 1.0),
        "w_ada": nrm(ks[2], (L, D_MODEL, N_MOD * D_MODEL), 0.5 * D_MODEL ** -0.5),
        "b_ada": nrm(ks[3], (L, N_MOD * D_MODEL), 0.01),
        "norm1_g": 1.0 + nrm(ks[4], (L, D_MODEL), 0.05),
        "w_in": nrm(ks[5], (L, D_MODEL, IN_COLS), D_MODEL ** -0.5),
        "ln_v_g": 1.0 + nrm(ks[6], (L, D_A), 0.05),
        "ln_v_b": nrm(ks[7], (L, D_A), 0.02),
        "w_spatial": nrm(ks[8], (L, N_HEADS_A, CHUNK, CHUNK), CHUNK ** -0.5),
        "b_spatial": 1.0 + nrm(ks[9], (L, N_HEADS_A, CHUNK), 0.1),
        "w_pool": nrm(ks[10], (L, N_POOL_GROUPS, POOL_GD, POOL_GD), POOL_GD ** -0.5),
        "b_pool": nrm(ks[11], (L, N_POOL_GROUPS, POOL_GD), 0.02),
        "pool_scale": 1.0 + nrm(ks[12], (L, D_B), 0.1),
        "b_gate": nrm(ks[13], (L, N_BRANCHES, D_MODEL), 0.1),
        "w_up_a": nrm(ks[14], (L, D_A, D_MODEL), D_A ** -0.5),
        "w_up_b": nrm(ks[15], (L, D_B, D_MODEL), D_B ** -0.5),
        "w_out": nrm(ks[16], (L, D_MODEL, D_MODEL), D_MODEL ** -0.5),
        "norm2_g": 1.0 + nrm(ks[17], (L, D_MODEL), 0.05),
        "w_ff1": nrm(ks[18], (L, D_MODEL, D_FF), D_MODEL ** -0.5),
        "w_ff2": nrm(ks[19], (L, D_FF, D_MODEL), D_FF ** -0.5),
        "norm_f_g": 1.0 + nrm(ks[20], (D_MODEL,), 0.05),
    }


def reference(x, c, w_ada, b_ada, norm1_g, w_in, ln_v_g, ln_v_b, w_spatial, b_spatial,
              w_pool, b_pool, pool_scale, b_gate, w_up_a, w_up_b, w_out, norm2_g,
              w_ff1, w_ff2, norm_f_g):
    c_act = jax.nn.silu(c)
    for l in range(DEPTH):
        mod = jnp.einsum('bd,de->be', c_act, w_ada[l]) + b_ada[l]
        shift1, scale1, gate1, shift2, scale2, gate2 = jnp.split(mod, N_MOD, axis=-1)
        h = modulate(rmsnorm(x, norm1_g[l]), shift1, scale1)
        y = token_mixer(h, w_in[l], ln_v_g[l], ln_v_b[l], w_spatial[l], b_spatial[l],
                        w_pool[l], b_pool[l], pool_scale[l], b_gate[l],
                        w_up_a[l], w_up_b[l], w_out[l])
        x = x + gate1[:, None, :] * y
        h = modulate(rmsnorm(x, norm2_g[l]), shift2, scale2)
        x = x + gate2[:, None, :] * channel_mixer(h, w_ff1[l], w_ff2[l])
    return rmsnorm(x, norm_f_g)
```

```python
import numpy as np
import concourse.bass as bass
import concourse.mybir as mybir
from concourse.bass_utils import run_bass_kernel_spmd

dt = mybir.dt
F32, BF16, F32R = dt.float32, dt.bfloat16, dt.float32r
AF = mybir.ActivationFunctionType
ALU = mybir.AluOpType
ES = {F32: 4, BF16: 2, F32R: 4}
K = 1024

NCORES = 8
TOK = 1024
TP = 512
D = 4096
DFF = 16384
NSLOT = 4
SLOT_B = 8 * K
EPS = 1e-6
GELU_C = 1.5957691216057308
USE_ACT_GELU = False
import os
KSTOP = os.environ.get("KSTOP", "")


class Arena:
    def __init__(self, name, t, base_dt):
        self.name, self.t, self.base_dt = name, t, base_dt


class Ref:
    __slots__ = ("ap", "arena", "lo", "hi")

    def __init__(self, ap, arena, lo, hi):
        self.ap, self.arena, self.lo, self.hi = ap, arena, lo, hi


class View:
    def __init__(self, arena, off, dtype, shape):
        self.arena, self.off, self.dtype, self.shape = arena, off, dtype, tuple(shape)
        es = ES[dtype]
        n = int(np.prod(shape))
        bes = ES[arena.base_dt]
        assert off % bes == 0 and (n * es) % bes == 0
        base = arena.t[:, off // bes:(off + n * es) // bes]
        ap = base.bitcast(dtype) if dtype != arena.base_dt else base
        if len(shape) == 2:
            ap = ap.rearrange("p (a b) -> p a b", b=shape[1])
        elif len(shape) == 3:
            ap = ap.rearrange("p (a b c) -> p a b c", b=shape[1], c=shape[2])
        self.full = ap
        st = []
        acc = 1
        for d_ in reversed(self.shape):
            st.append(acc)
            acc *= d_
        self.strides = tuple(reversed(st))

    def __call__(self, *idx, p=None):
        idx = tuple(idx) + (slice(None),) * (len(self.shape) - len(idx))
        lo = hi = 0
        for i, dim, st in zip(idx, self.shape, self.strides):
            if isinstance(i, int):
                assert 0 <= i < dim, (i, dim)
                lo += i * st
                hi += i * st
            else:
                a, b, _ = i.indices(dim)
                assert b > a
                lo += a * st
                hi += (b - 1) * st
        es = ES[self.dtype]
        psl = slice(None) if p is None else slice(p[0], p[1])
        ap = self.full[(psl,) + idx]
        return Ref(ap, self.arena, self.off + lo * es, self.off + (hi + 1) * es)


class DramArena:
    def __init__(self, name):
        self.name = name


def dref(ap, arena, lo=0, hi=1):
    return Ref(ap, arena, lo, hi)


class Op:
    __slots__ = ("eng", "stream", "fn", "deps", "needs_inc", "ticket", "is_dma", "idx")


class Rec:
    __slots__ = ("lo", "hi", "writer", "readers")


class Prog:
    ENGS = ("pe", "act", "dve", "pool", "sp")

    def __init__(self):
        self.ops = []
        self.recs = {}
        self.last_chan = {}
        self.dry = False

    def add(self, eng, fn, reads=(), writes=(), chan=None):
        if self.dry:
            return None
        op = Op()
        op.eng, op.fn, op.deps, op.needs_inc, op.ticket = eng, fn, {}, False, 0
        op.is_dma = chan is not None
        op.stream = ("dma:" + chan) if chan is not None else eng
        op.idx = len(self.ops)
        if chan is not None:
            prev = self.last_chan.get(chan)
            if prev is not None:
                op.deps[prev.idx] = prev
            self.last_chan[chan] = op
        for r in reads:
            self._access(op, r, False)
        for w in writes:
            self._access(op, w, True)
        self.ops.append(op)
        return op

    def _dep(self, op, other):
        if other is op:
            return
        if other.stream == "pe" and op.stream == "pe":
            return
        op.deps[other.idx] = other

    def _access(self, op, ref, is_write):
        name = ref.arena.name
        if name == "psum":
            ref = Ref(ref.ap, ref.arena, ref.lo // 2048 * 2048, (ref.hi + 2047) // 2048 * 2048)
            is_write = True
        recs = self.recs.get(name, [])
        new = []
        for r in recs:
            if r.hi <= ref.lo or r.lo >= ref.hi:
                new.append(r)
                continue
            if r.writer is not None:
                self._dep(op, r.writer)
            if is_write:
                for rd in r.readers.values():
                    self._dep(op, rd)
                if ref.lo <= r.lo and r.hi <= ref.hi:
                    continue
                new.append(r)
            else:
                r.readers[op.stream] = op
                new.append(r)
        if is_write:
            r = Rec()
            r.lo, r.hi, r.writer, r.readers = ref.lo, ref.hi, op, {}
            new.append(r)
        self.recs[name] = new

    def emit(self, nc, block, sems):
        for op in self.ops:
            for d_ in op.deps.values():
                d_.needs_inc = True
        cnt = {}
        for op in self.ops:
            if op.is_dma:
                cnt[op.stream] = cnt.get(op.stream, 0) + 16
                op.ticket = cnt[op.stream]
            elif op.needs_inc:
                cnt[op.stream] = cnt.get(op.stream, 0) + 1
                op.ticket = cnt[op.stream]
        self.max_counts = cnt

        def run(engname):
            def body(e):
                waited = {}
                for op in self.ops:
                    if op.eng != engname:
                        continue
                    for d_ in op.deps.values():
                        if waited.get(d_.stream, 0) < d_.ticket:
                            e.wait_ge(sems[d_.stream], d_.ticket)
                            waited[d_.stream] = d_.ticket
                    if op.fn is None:
                        continue
                    ins = op.fn(e)
                    if op.is_dma:
                        ins.then_inc(sems[op.stream], 16)
                    elif op.needs_inc:
                        ins.then_inc(sems[op.stream], 1)
            return body

        block.tensor(run("pe"))
        block.scalar(run("act"))
        block.vector(run("dve"))
        block.gpsimd(run("pool"))
        block.sync(run("sp"))

    def streams(self):
        s = []
        for op in self.ops:
            if op.stream not in s:
                s.append(op.stream)
        return s


def build_program():
    nc = bass.Bass("TRN2", target_bir_lowering=False)

    def din(name, shape):
        return nc.dram_tensor(name, list(shape), F32, kind="ExternalInput").ap()

    x_d = din("x", [TOK, D])
    xh_d = din("xh", [32, D])
    hmask_d = din("hmask", [128, 2])
    invc_d = din("invc", [2 * 128, 4 * TP])
    vecs_d = din("vecs", [224, 128])
    bada_d = din("bada", [192, 128])
    identf_d = din("identf", [128, 128])
    tril_d = din("tril", [128, 128])
    lng_d = din("lng", [128, 2048])
    lnb_d = din("lnb", [128, 2048])
    bsp_d = din("bsp", [128, 2048])
    wada_d = din("w_ada", [D, 6 * D])
    wt_in_d = din("wt_in", [2 * 56 * 128, 4096])
    wt_v8_d = din("wt_v8", [16 * 128, 4096])
    wsp_d = din("w_sp", [16, 128, 128])
    wpool_d = din("w_pool", [2048, 512])
    wt_upa_d = din("wt_upa", [16 * 128, 4096])
    wt_upb_d = din("wt_upb", [16 * 128, 4096])
    wt_out_d = din("wt_out", [32 * 128, 4096])
    wt_ff1_d = din("wt_ff1", [128 * 128, 4096])
    wt_ff2_d = din("wt_ff2", [128 * 128, 4096])
    TILED = {"in": (wt_in_d, 56), "v8": (wt_v8_d, 4), "upa": (wt_upa_d, 16), "upb": (wt_upb_d, 16),
             "out": (wt_out_d, 16), "ff1": (wt_ff1_d, 64), "ff2": (wt_ff2_d, 16)}
    win_d, wupa_d, wupb_d, wout_d, wff1_d, wff2_d = "in", "upa", "upb", "out", "ff1", "ff2"
    y_d = nc.dram_tensor("y", [TOK, D], F32, kind="ExternalOutput").ap()
    dbg_d = nc.dram_tensor("dbg", [128, 8192], F32, kind="ExternalOutput").ap() if KSTOP else None
    DDBG = DramArena("dbg")
    x2d = nc.dram_tensor("x2d", [2 * 32 * 128, TP], F32).ap()
    modd = nc.dram_tensor("modd", [192, 128], F32).ap()
    DX2 = DramArena("x2d")
    DMOD = DramArena("modd")
    DY = DramArena("y")
    DIN = DramArena("in")

    P = Prog()
    PERSIST_B = 10 * K
    SCR_B = 2 * K
    BIG_B = 160 * K

    import contextlib
    with contextlib.ExitStack() as es:
        persist_t = es.enter_context(nc.sbuf_tensor("persist", [128, PERSIST_B // 4], F32))
        scr_t = es.enter_context(nc.sbuf_tensor("scr", [128, SCR_B // 2], BF16))
        ring_t = es.enter_context(nc.sbuf_tensor("ring", [128, NSLOT * SLOT_B // 2], BF16))
        big_t = es.enter_context(nc.sbuf_tensor("big", [128, BIG_B // 2], BF16))
        psum_t = es.enter_context(nc.psum_tensor("ps", [128, 4096], F32))
        APER = Arena("persist", persist_t, F32)
        ASCR = Arena("scr", scr_t, BF16)
        ARING = Arena("ring", ring_t, BF16)
        ABIG = Arena("big", big_t, BF16)
        APS = Arena("psum", psum_t, F32)

        off = [0]

        def pv(dtype, shape):
            n = int(np.prod(shape)) * ES[dtype]
            n = (n + 31) // 32 * 32
            v = View(APER, off[0], dtype, shape)
            off[0] += n
            assert off[0] <= PERSIST_B
            return v

        IDF = pv(F32, (128,))
        IDB = pv(BF16, (128,))
        ONESB = pv(BF16, (128,))
        VECT = pv(F32, (224,))
        BADAT = pv(F32, (192,))
        MODT = pv(F32, (192,))
        A1 = pv(F32, (32,))
        A2 = pv(F32, (32,))
        CACT = pv(F32, (32,))
        WMT = pv(BF16, (16, 128))
        HMASK = pv(F32, (2,))
        SS = pv(F32, (1,))
        SS2 = pv(F32, (1,))
        RSTD1 = pv(F32, (1,))
        MV = pv(F32, (2,))
        VS = pv(F32, (1,))
        VR = pv(F32, (1,))
        NB = pv(F32, (1,))
        STATS = pv(F32, (4, 4, 6))
        C_G1, C_G2, C_GF, C_BP, C_PS, C_BGA, C_BGB = 32, 64, 96, 128, 144, 160, 192
        M_SH1, M_SC1, M_GT1, M_SH2, M_SC2, M_GT2 = 0, 32, 64, 96, 128, 160

        RING = [View(ARING, s * SLOT_B, BF16, (16, 256)) for s in range(NSLOT)]
        RINGW = [View(ARING, s * SLOT_B, BF16, (8, 512)) for s in range(NSLOT)]
        RL = [View(ASCR, i * K, BF16, (TP,)) for i in range(2)]

        def bv(offk, dtype, shape):
            return View(ABIG, int(offk * K), dtype, shape)

        H1T = bv(0, BF16, (32, TP))
        HALO = bv(32, BF16, (32, 16))
        MIXB = bv(33, BF16, (16, TP))
        YB = bv(49, BF16, (16, TP))
        MERGED = bv(65, BF16, (32, TP))
        XT = [bv(97, F32, (D,)), bv(113, F32, (D,))]
        XN = bv(129, BF16, (D,))
        GVB = bv(97, BF16, (4, 2048))
        ZT = bv(113, F32, (2048,))
        LNG = bv(121, F32, (2048,))
        LNB = bv(129, F32, (2048,))
        BSP = bv(137, F32, (16, 128))
        GT = [bv(145, F32, (256,)), bv(146, F32, (256,)), bv(147, F32, (TP,)), bv(149, F32, (TP,))]
        WPOOL = bv(97, BF16, (16, 512))
        POOLED = bv(113, BF16, (16, TP))
        INVC = bv(129, F32, (4, TP))
        PT = [bv(137, F32, (528,)), bv(137 + 2.125, F32, (528,))]
        SA = bv(137 + 4.25, F32, (528,))
        SB = bv(137 + 6.375, F32, (528,))
        SGA = bv(97, F32, (2, TP))
        SGB = bv(101, F32, (2, TP))
        MP = bv(105, F32, (2, TP))
        X2T = bv(0, F32, (32, TP))
        XC = [bv(97, F32, (4, 256)), bv(101, F32, (4, 256))]
        TT5 = [bv(105, F32, (TP,)), bv(107, F32, (TP,))]
        SQ5 = [bv(109, BF16, (TP,)), bv(110, BF16, (TP,)), bv(117, BF16, (TP,)), bv(118, BF16, (TP,))]
        RSTD2 = bv(111, F32, (TP,))
        TMP6 = [bv(113, F32, (TP,)), bv(115, F32, (TP,)), bv(119, F32, (TP,)), bv(121, F32, (TP,))]
        H2T = bv(128, BF16, (32, TP))
        AT = bv(0, BF16, (128, TP))
        TT8 = [bv(128, F32, (TP,)), bv(130, F32, (TP,))]
        XS = [bv(132, F32, (TP,)), bv(134, F32, (TP,))]
        X3S = [bv(136, F32, (TP,)), bv(138, F32, (TP,))]
        SQ8 = [bv(140, BF16, (TP,)), bv(141, BF16, (TP,)), bv(144, BF16, (TP,)), bv(145, BF16, (TP,))]
        RSTD3 = bv(142, F32, (TP,))
        OT = bv(33, F32, (4, D))
        XS9 = [bv(137 + 2 * i, F32, (TP,)) for i in range(6)]
        OS9 = [bv(149 + 2 * i, F32, (TP,)) for i in range(4)]
        RSTD9 = bv(157, F32, (TP,))
        CBC = bv(0, F32, (32, 128))
        WA = [bv(100 + 8 * i, F32, (2048,)) for i in range(3)]
        WAB = [bv(124 + 4 * i, BF16, (2048,)) for i in range(3)]
        CBCB = bv(40, BF16, (32, 128))
        WAC = [bv(136 + 4 * i, BF16, (2048,)) for i in range(3)]
        MODROW = bv(64, F32, (D,))
        WSP = bv(80, F32, (16, 128))
        RA = [bv(88 + 0.5 * i, F32, (128,)) for i in range(4)]
        TRIL = bv(90, F32, (128,))
        ONESF = bv(90.5, F32, (128,))

        def bank(b, dtype=F32, shape=(512,)):
            return View(APS, b * 2048, dtype, shape)

        def hbank(hb, dtype=F32, shape=(256,)):
            return View(APS, hb * 1024, dtype, shape)

        rot = {"acc": 0, "half": 0}

        def acc_bank():
            b = rot["acc"] % 6
            rot["acc"] += 1
            return b

        def acc_half():
            b = rot["half"] % 12
            rot["half"] += 1
            return b

        def dma(eng, chan, out, in_):
            P.add(eng, lambda e, o=out.ap, i=in_.ap: e.dma_start(out=o, in_=i),
                  reads=[in_], writes=[out], chan=chan)

        def act(out, in_, func, scale=1.0, bias=0.0, accum=None, extra_reads=()):
            def fn(e, o=out.ap, i=in_.ap):
                kw = {}
                if accum is not None:
                    kw["accum_out"] = accum.ap
                sc = scale.ap if isinstance(scale, Ref) else float(scale)
                bi = bias.ap if isinstance(bias, Ref) else float(bias)
                return e.activation(o, i, func, bias=bi, scale=sc, **kw)
            rd = [in_] + list(extra_reads)
            if isinstance(scale, Ref):
                rd.append(scale)
            if isinstance(bias, Ref):
                rd.append(bias)
            wr = [out] + ([accum] if accum is not None else [])
            P.add("act", fn, reads=rd, writes=wr)

        def ts(out, in0, s1, s2, op0, op1=None, eng="dve"):
            def fn(e, o=out.ap, i=in0.ap):
                a = s1.ap if isinstance(s1, Ref) else s1
                b = s2.ap if isinstance(s2, Ref) else s2
                if op1 is None:
                    return e.tensor_scalar(o, i, a, None, op0)
                return e.tensor_scalar(o, i, a, b, op0, op1)
            rd = [in0] + [s for s in (s1, s2) if isinstance(s, Ref)]
            P.add(eng, fn, reads=rd, writes=[out])

        def tt(out, in0, in1, op, eng="dve"):
            P.add(eng, lambda e, o=out.ap, a=in0.ap, b=in1.ap: e.tensor_tensor(o, a, b, op),
                  reads=[in0, in1], writes=[out])

        def stt(out, in0, scalar, in1, op0, op1):
            def fn(e, o=out.ap, a=in0.ap, b=in1.ap):
                s = scalar.ap if isinstance(scalar, Ref) else scalar
                return e.scalar_tensor_tensor(o, a, s, b, op0, op1)
            rd = [in0, in1] + ([scalar] if isinstance(scalar, Ref) else [])
            P.add("dve", fn, reads=rd, writes=[out])

        def recip(out, in_):
            P.add("dve", lambda e, o=out.ap, i=in_.ap: e.reciprocal(o, i), reads=[in_], writes=[out])

        def copy(out, in_, eng="dve"):
            P.add(eng, lambda e, o=out.ap, i=in_.ap: e.tensor_copy(o, i), reads=[in_], writes=[out])

        def memset(out, val, eng="dve"):
            P.add(eng, lambda e, o=out.ap: e.memset(o, val), writes=[out])

        def mm_group(out, pairs, start, stop, reads):
            n = len(pairs)

            def fn(e, o=out.ap):
                ins = None
                for q, (l, r) in enumerate(pairs):
                    ins = e.matmul(o, l, r, start=(start and q == 0), stop=(stop and q == n - 1))
                return ins
            P.add("pe", fn, reads=reads, writes=[out])

        def transposes(items, reads, writes):
            def fn(e):
                ins = None
                for (o, i, idn) in items:
                    ins = e.transpose(o, i, idn)
                return ins
            P.add("pe", fn, reads=reads, writes=writes)

        def gelu_evac(src, dst, tA):
            if USE_ACT_GELU:
                act(dst, src, AF.Gelu_apprx_tanh)
                return
            act(tA, src, AF.Square)
            ts(tA, tA, 0.044715, 1.0, ALU.mult, ALU.add)
            tt(tA, tA, src, ALU.mult)
            act(tA, tA, AF.Sigmoid, scale=GELU_C)
            tt(dst, tA, src, ALU.mult)

        ring = {"specs": [], "next_get": 0, "next_load": 0}

        def ring_load(n):
            wd, r0, c0, nk, ncols = ring["specs"][n]
            s = n % NSLOT
            if ncols == 512:
                tap, ncb = TILED["v8"]
                blk = (r0 // 1024) * ncb + (c0 - 2048) // 512
            else:
                tap, ncb = TILED[wd]
                blk = (r0 // 2048) * ncb + c0 // 256
            src = tap[blk * 128:(blk + 1) * 128, 0:nk * ncols].rearrange("p (k c) -> p k c", c=ncols)
            dst = RING[s] if ncols == 256 else RINGW[s]
            dma("pool", "ring%d" % s, dst(slice(0, nk)), dref(src, DIN))

        def ring_get(wd, r0, c0, nk=16, ncols=256):
            n = ring["next_get"]
            ring["next_get"] += 1
            ret = RING[n % NSLOT] if ncols == 256 else RINGW[n % NSLOT]
            if P.dry:
                ring["specs"].append((wd, r0, c0, nk, ncols))
                return ret
            if n == 0:
                for m in range(min(NSLOT, len(ring["specs"]))):
                    ring_load(m)
                ring["next_load"] = NSLOT
            assert ring["specs"][n][1:] == (r0, c0, nk, ncols)
            return ret

        def ring_release():
            if P.dry:
                return
            m = ring["next_load"]
            if m < len(ring["specs"]):
                ring_load(m)
            ring["next_load"] += 1

        def load_rows_T(src_d, nrows, dstT, col0, arena_d=DIN):
            done = 0
            q = 0
            while done < nrows:
                n = min(128, nrows - done)
                ra = RA[q % 4]
                dma("sp", "ra%d" % (q % 4), ra(p=(0, n)), dref(src_d[done:done + n, :], arena_d))
                b = 6 + (q % 2)
                pb = bank(b, F32, (128,))
                transposes([(pb(slice(0, n)).ap, ra(p=(0, n)).ap, IDF(slice(0, n), p=(0, n)).ap)],
                           reads=[ra(p=(0, n)), IDF()], writes=[pb(slice(0, n))])
                copy(dstT(slice(col0 + done, col0 + done + n)), pb(slice(0, n)))
                done += n
                q += 1

        def setup():
            dma("sp", "c0", IDF(), dref(identf_d, DIN))
            dma("sp", "c1", TRIL(), dref(tril_d, DIN))
            dma("sp", "c2", HMASK(), dref(hmask_d, DIN))
            memset(ONESF(), 1.0)
            memset(ONESB(), 1.0)
            copy(IDB(), IDF())
            load_rows_T(vecs_d, 224, VECT, 0)
            load_rows_T(bada_d, 192, BADAT, 0)
            act(CACT(), VECT(slice(0, 32)), AF.Sigmoid)
            tt(CACT(), CACT(), VECT(slice(0, 32)), ALU.mult)
            for kc in range(32):
                ts(CBC(kc), ONESF(), CACT(slice(kc, kc + 1)), None, ALU.mult)
            copy(CBCB(), CBC())
            st = {"qf": 0, "qb": 0}

            def mod_groups(g0, g1):
                for g in range(g0, g1):
                    for hf in range(2):
                        c0 = g * D + hf * 2048
                        for kc in range(32):
                            src = dref(wada_d[kc * 128:(kc + 1) * 128, c0:c0 + 2048], DIN)
                            if kc % 2 == 0:
                                qf = st["qf"]
                                wf = WA[qf % 3]
                                dma("sp", "wa%d" % (qf % 3), wf(), src)
                                wa = WAC[qf % 3]
                                if qf % 2 == 0:
                                    act(wa(), wf(), AF.Copy)
                                else:
                                    copy(wa(), wf())
                                st["qf"] += 1
                            else:
                                qb = st["qb"]
                                wa = WAB[qb % 3]
                                dma("pool", "wab%d" % (qb % 3), wa(), src)
                                st["qb"] += 1
                            for b in range(4):
                                mm_group(bank(b)(), [(CBCB(kc).ap, wa(slice(b * 512, (b + 1) * 512)).ap)],
                                         start=(kc == 0), stop=(kc == 31), reads=[CBCB(kc), wa()])
                        for b in range(4):
                            act(MODROW(slice(hf * 2048 + b * 512, hf * 2048 + (b + 1) * 512), p=(0, 1)),
                                bank(b)(p=(0, 1)), AF.Copy)
                    dst = modd[g * 32:(g + 1) * 32, :].rearrange("(o r) c -> o (r c)", o=1)
                    dma("sp", "modst", dref(dst, DMOD, g, g + 1), MODROW(p=(0, 1)))

            def mod_finalize(r0, n, q2):
                ra = RA[q2]
                dma("sp", "ra%d" % q2, ra(p=(0, n)), dref(modd[r0:r0 + n, :], DMOD, r0 // 32, (r0 + n) // 32))
                pb = bank(6 + (q2 % 2), F32, (128,))
                transposes([(pb(slice(0, n)).ap, ra(p=(0, n)).ap, IDF(slice(0, n), p=(0, n)).ap)],
                           reads=[ra(p=(0, n)), IDF()], writes=[pb(slice(0, n))])
                tt(MODT(slice(r0, r0 + n)), pb(slice(0, n)), BADAT(slice(r0, r0 + n)), ALU.add)

            mod_groups(0, 2)
            mod_finalize(0, 64, 0)
            stt(A1(), MODT(slice(M_SC1, M_SC1 + 32)), 1.0, VECT(slice(C_G1, C_G1 + 32)), ALU.add, ALU.mult)
            for _ in phase0(0):
                pass
            mod_groups(2, 6)
            mod_finalize(64, 128, 1)
            stt(A2(), MODT(slice(M_SC2, M_SC2 + 32)), 1.0, VECT(slice(C_G2, C_G2 + 32)), ALU.add, ALU.mult)
            dma("sp", "c3", WSP(), dref(wsp_d.rearrange("h t s -> t h s"), DIN))
            for hd in range(16):
                tt(WSP(hd), WSP(hd), TRIL(), ALU.mult)
            for g in range(4):
                pb = bank(6 + (g % 2), F32, (4, 128))
                transposes([(pb(q).ap, WSP(g * 4 + q).ap, IDF().ap) for q in range(4)],
                           reads=[WSP(slice(g * 4, g * 4 + 4)), IDF()], writes=[pb()])
                copy(WMT(slice(g * 4, g * 4 + 4)), pb())

        def phase0(h):
            T0 = h * TP
            for i in range(5):
                halo = (i == 4)
                n = 16 if halo else 128
                pp = (0, n)
                xt = XT[i % 2]
                src = xh_d[h * 16:(h + 1) * 16, :] if halo else x_d[T0 + i * 128:T0 + (i + 1) * 128, :]
                dma("sp", "xt%d" % (i % 2), xt(p=pp), dref(src, DIN))
                act(XN(p=pp), xt(p=pp), AF.Square, accum=SS(p=pp))
                act(SS2(p=pp), SS(p=pp), AF.Sqrt, scale=1.0 / D, bias=EPSV(p=pp))
                recip(RSTD1(p=pp), SS2(p=pp))
                act(XN(p=pp), xt(p=pp), AF.Identity, scale=RSTD1(p=pp))
                for g in range(4):
                    pb = bank(6 + (g % 2), BF16, (8, 128))
                    items = []
                    for q in range(8):
                        kc = g * 8 + q
                        items.append((pb(q, slice(0, n)).ap, XN(slice(kc * 128, (kc + 1) * 128), p=pp).ap,
                                      IDB(slice(0, n), p=pp).ap))
                    transposes(items, reads=[XN(p=pp), IDB()], writes=[pb()])
                    for q in range(8):
                        kc = g * 8 + q
                        dst = HALO(kc) if halo else H1T(kc, slice(i * 128, (i + 1) * 128))
                        ts(dst, pb(q, slice(0, n)), A1(slice(kc, kc + 1)), MODT(slice(M_SH1 + kc, M_SH1 + kc + 1)),
                           ALU.mult, ALU.add)
                yield

        def phase1(h):
            dma("sp", "lng", LNG(), dref(lng_d, DIN))
            dma("sp", "lnb", LNB(), dref(lnb_d, DIN))
            dma("sp", "bsp", BSP(), dref(bsp_d.rearrange("p (a b) -> p a b", b=128), DIN))
            for cb in range(4):
                c0 = 2048 + cb * 512
                pbs = [bank(acc_bank()) for _ in range(4)]
                for kb in range(4):
                    slot = ring_get(win_d, kb * 1024, c0, 8, 512)
                    for i in range(4):
                        pairs = [(H1T(kb * 8 + k, slice(i * 128, (i + 1) * 128)).ap, slot(k).ap) for k in range(8)]
                        mm_group(pbs[i](), pairs, start=(kb == 0), stop=(kb == 3),
                                 reads=[slot(), H1T(slice(kb * 8, kb * 8 + 8))])
                    ring_release()
                for i in range(4):
                    gdst = GVB(i, slice(cb * 512, (cb + 1) * 512))
                    gelu_evac(pbs[i](), gdst, GT[2 + (i % 2)]())
                    P.add("dve", lambda e, o=STATS(i, cb).ap, a=gdst.ap: e.bn_stats(o, a),
                          reads=[gdst], writes=[STATS(i, cb)])
            chk("p1a", [(GVB(0, slice(0, 512)), 512), (STATS(0), 24)])
            for i in range(4):
                P.add("dve", lambda e, o=MV().ap, a=STATS(i).ap: e.bn_aggr(o, a), reads=[STATS(i)], writes=[MV()])
                act(VS(), MV(slice(1, 2)), AF.Sqrt, scale=1.0, bias=EPSV())
                recip(VR(), VS())
                stt(NB(), MV(slice(0, 1)), -1.0, VR(), ALU.mult, ALU.mult)
                act(ZT(), GVB(i), AF.Identity, scale=VR(), bias=NB())
                tt(ZT(), ZT(), LNG(), ALU.mult)
                tt(GVB(i), ZT(), LNB(), ALU.add)
                if i == 0:
                    chk("p1b", [(GVB(0, slice(0, 512)), 512), (MV(), 2)])
                for g in range(4):
                    pb = bank(6 + (g % 2), F32, (4, 128))
                    for q in range(4):
                        hd = g * 4 + q
                        mm_group(pb(q), [(GVB(i, slice(hd * 128, (hd + 1) * 128)).ap, WMT(hd).ap)], True, True,
                                 reads=[GVB(i), WMT(hd)])
                    tt(MIXB(slice(g * 4, g * 4 + 4), slice(i * 128, (i + 1) * 128)), pb(),
                       BSP(slice(g * 4, g * 4 + 4)), ALU.add)

        def fm_proj(wd, c0, nkb, act_view, nchunks_rows=16):
            banks = [bank(acc_bank()) for _ in range(2)]
            for kb in range(nkb):
                slot = ring_get(wd, kb * 2048, c0, nchunks_rows)
                for jl in range(2):
                    pairs = [(slot(k, slice(jl * 128, (jl + 1) * 128)).ap, act_view(kb * 16 + k).ap)
                             for k in range(nchunks_rows)]
                    mm_group(banks[jl](), pairs, start=(kb == 0), stop=(kb == nkb - 1),
                             reads=[slot(), act_view(slice(kb * 16, kb * 16 + nchunks_rows))])
                ring_release()
            return banks

        def phase2(h):
            for cb in range(8):
                banks = fm_proj(win_d, cb * 256, 2, H1T)
                for jl in range(2):
                    j = cb * 2 + jl
                    tmp = GT[2 + jl]
                    t2 = TT5[jl]
                    gelu_evac(banks[jl](), t2(), tmp())
                    tt(MIXB(j), t2(), MIXB(j), ALU.mult)

        def phase3(h):
            dma("pool", "wpool", WPOOL(), dref(wpool_d.rearrange("(a p) d -> p a d", p=128), DIN))
            dma("sp", "invc", INVC(), dref(invc_d[h * 128:(h + 1) * 128, :].rearrange("p (a b) -> p a b", b=TP), DIN))
            for cb in range(8):
                c0 = 4096 + cb * 256
                pbs = [bank(acc_bank()) for _ in range(2)]
                hbs = [bank(6 + jl, F32, (16,)) for jl in range(2)]
                for kb in range(2):
                    slot = ring_get(win_d, kb * 2048, c0)
                    for jl in range(2):
                        pairs = [(slot(k, slice(jl * 128, (jl + 1) * 128)).ap, H1T(kb * 16 + k).ap) for k in range(16)]
                        mm_group(pbs[jl](), pairs, start=(kb == 0), stop=(kb == 1),
                                 reads=[slot(), H1T(slice(kb * 16, kb * 16 + 16))])
                        pairs = [(slot(k, slice(jl * 128, (jl + 1) * 128)).ap, HALO(kb * 16 + k).ap) for k in range(16)]
                        mm_group(hbs[jl](), pairs, start=(kb == 0), stop=(kb == 1),
                                 reads=[slot(), HALO(slice(kb * 16, kb * 16 + 16))])
                    ring_release()
                for jl in range(2):
                    j = cb * 2 + jl
                    g = j // 4
                    pb = pbs[jl]
                    hb = hbs[jl]
                    pt = PT[j % 2]
                    act(pt(slice(16, 528)), pb(), AF.Copy)
                    ts(pt(slice(0, 16)), hb(), HMASK(slice(h, h + 1)), None, ALU.mult)
                    tt(SA(slice(1, 528)), pt(slice(1, 528)), pt(slice(0, 527)), ALU.add)
                    cur = SA
                    if g >= 1:
                        tt(SB(slice(3, 528)), SA(slice(3, 528)), SA(slice(1, 526)), ALU.add)
                        cur = SB
                    if g >= 2:
                        tt(SA(slice(7, 528)), SB(slice(7, 528)), SB(slice(3, 524)), ALU.add)
                        cur = SA
                    if g >= 3:
                        tt(SB(slice(15, 528)), SA(slice(15, 528)), SA(slice(7, 520)), ALU.add)
                        cur = SB
                    oth = SB if cur is SA else SA
                    tt(oth(slice(16, 528)), cur(slice(16, 528)), INVC(g), ALU.mult)
                    tt(POOLED(j), oth(slice(16, 528)), pt(slice(16, 528)), ALU.subtract)
            for g in range(4):
                for dj in range(4):
                    pb = bank(acc_bank())
                    pairs = [(WPOOL(g * 4 + cc, slice(dj * 128, (dj + 1) * 128)).ap, POOLED(g * 4 + cc).ap)
                             for cc in range(4)]
                    mm_group(pb(), pairs, True, True, reads=[WPOOL(), POOLED(slice(g * 4, g * 4 + 4))])
                    j = g * 4 + dj
                    ts(YB(j), pb(), VECT(slice(C_BP + j, C_BP + j + 1)), VECT(slice(C_PS + j, C_PS + j + 1)),
                       ALU.add, ALU.mult)

        def phase4(h):
            for jp in range(16):
                banks = fm_proj(win_d, 6144 + jp * 256, 2, H1T)
                for jl in range(2):
                    j = jp * 2 + jl
                    act(SGA(jl), banks[jl](), AF.Sigmoid, bias=VECT(slice(C_BGA + j, C_BGA + j + 1)))
                banks = fm_proj(win_d, 10240 + jp * 256, 2, H1T)
                for jl in range(2):
                    j = jp * 2 + jl
                    act(SGB(jl), banks[jl](), AF.Sigmoid, bias=VECT(slice(C_BGB + j, C_BGB + j + 1)))
                banks = fm_proj(wupa_d, jp * 256, 1, MIXB)
                for jl in range(2):
                    tt(MP(jl), banks[jl](), SGA(jl), ALU.mult)
                banks = fm_proj(wupb_d, jp * 256, 1, YB)
                for jl in range(2):
                    j = jp * 2 + jl
                    tt(SGB(jl), banks[jl](), SGB(jl), ALU.mult)
                    tt(MERGED(j), MP(jl), SGB(jl), ALU.add)

        pend_stats = []

        def stats_mm(sq, j):
            pend_stats.append((sq, j))

        def flush_stats():
            for (sq, j) in pend_stats:
                mm_group(bank(6)(), [(ONESB().ap, sq.ap)], start=(j == 0), stop=(j == 31), reads=[ONESB(), sq])
            del pend_stats[:]

        def phase5(h):
            T0 = h * TP

            def load_xc(jp):
                src = x_d[T0:T0 + TP, jp * 256:(jp + 1) * 256].rearrange("(i p) c -> p i c", p=128)
                dma("sp", "xc%d" % (jp % 2), XC[jp % 2](), dref(src, DIN))

            load_xc(0)
            for jp in range(16):
                xc = XC[jp % 2]
                xbs = []
                for jl in range(2):
                    xb = bank(acc_bank(), F32, (4, 128))
                    transposes([(xb(i).ap, xc(i, slice(jl * 128, (jl + 1) * 128)).ap, IDF().ap) for i in range(4)],
                               reads=[xc(), IDF()], writes=[xb()])
                    xbs.append(xb)
                if jp + 1 < 16:
                    load_xc(jp + 1)
                banks = fm_proj(wout_d, jp * 256, 2, MERGED)
                flush_stats()
                for jl in range(2):
                    j = jp * 2 + jl
                    act(TT5[jl](), banks[jl](), AF.Identity, scale=MODT(slice(M_GT1 + j, M_GT1 + j + 1)))
                for jl in range(2):
                    j = jp * 2 + jl
                    tt(X2T(j), xbs[jl](), TT5[jl](), ALU.add)
                for jl in range(2):
                    j = jp * 2 + jl
                    sq = SQ5[(jp % 2) * 2 + jl]
                    act(sq(), X2T(j), AF.Square)
                    stats_mm(sq(), j)
                for jl in range(2):
                    j = jp * 2 + jl
                    dst = x2d[(h * 32 + j) * 128:(h * 32 + j + 1) * 128, :]
                    dma("sp", "x2st%d" % (j % 2), dref(dst, DX2, h * 32 + j, h * 32 + j + 1), X2T(j))

        def phase6(h):
            flush_stats()
            act(RSTD2(), bank(6)(), AF.Sqrt, scale=1.0 / D, bias=EPSV())
            recip(RSTD2(), RSTD2())
            for j in range(32):
                t6 = TMP6[j % 4]
                stt(t6(), X2T(j), A2(slice(j, j + 1)), RSTD2(), ALU.mult, ALU.mult)
                act(H2T(j), t6(), AF.Identity, bias=MODT(slice(M_SH2 + j, M_SH2 + j + 1)))

        def phase7(h):
            for cb in range(64):
                banks = fm_proj(wff1_d, cb * 256, 2, H2T)
                for jl in range(2):
                    j = cb * 2 + jl
                    rl = RL[jl]
                    act(rl(), banks[jl](), AF.Relu)
                    tt(AT(j), rl(), rl(), ALU.mult)

        def phase8(h):
            for cb in range(16):
                c0 = cb * 256
                pbs = [bank(acc_bank()) for _ in range(2)]
                for kb in range(8):
                    slot = ring_get(wff2_d, kb * 2048, c0)
                    for jl in range(2):
                        pairs = [(slot(k, slice(jl * 128, (jl + 1) * 128)).ap, AT(kb * 16 + k).ap) for k in range(16)]
                        mm_group(pbs[jl](), pairs, start=(kb == 0), stop=(kb == 7),
                                 reads=[slot(), AT(slice(kb * 16, kb * 16 + 16))])
                    ring_release()
                    if kb == 3:
                        flush_stats()
                rows = [h * 32 + cb * 2 + jl for jl in range(2)]
                srcs = [x2d[r * 128:(r + 1) * 128, :] for r in rows]
                for jl in range(2):
                    dma("sp", "xs%d" % jl, XS[jl](), dref(srcs[jl], DX2, rows[jl], rows[jl] + 1))
                for jl in range(2):
                    j = cb * 2 + jl
                    act(TT8[jl](), pbs[jl](), AF.Identity, scale=MODT(slice(M_GT2 + j, M_GT2 + j + 1)))
                for jl in range(2):
                    tt(X3S[jl](), XS[jl](), TT8[jl](), ALU.add)
                for jl in range(2):
                    j = cb * 2 + jl
                    sq = SQ8[(cb % 2) * 2 + jl]
                    act(sq(), X3S[jl](), AF.Square)
                    stats_mm(sq(), j)
                for jl in range(2):
                    dma("sp", "x3st%d" % jl, dref(srcs[jl], DX2, rows[jl], rows[jl] + 1), X3S[jl]())

        def phase9(h, out_ops):
            T0 = h * TP
            flush_stats()
            act(RSTD9(), bank(6)(), AF.Sqrt, scale=1.0 / D, bias=EPSV())
            recip(RSTD9(), RSTD9())
            for j in range(32):
                xs = XS9[j % 6]
                row = (h * 32 + j)
                dma("sp", "xs9%d" % (j % 6), xs(), dref(x2d[row * 128:(row + 1) * 128, :], DX2, row, row + 1))
                os_ = OS9[j % 4]
                stt(os_(), xs(), VECT(slice(C_GF + j, C_GF + j + 1)), RSTD9(), ALU.mult, ALU.mult)
                pb = bank(acc_bank(), F32, (4, 128))
                transposes([(pb(i).ap, os_(slice(i * 128, (i + 1) * 128)).ap, IDF().ap) for i in range(4)],
                           reads=[os_(), IDF()], writes=[pb()])
                if j % 2 == 0:
                    act(OT(slice(0, 4), slice(j * 128, (j + 1) * 128)), pb(), AF.Copy)
                else:
                    copy(OT(slice(0, 4), slice(j * 128, (j + 1) * 128)), pb())
                if j % 16 == 15:
                    hf = j // 16
                    for i in range(4):
                        dst = y_d[T0 + i * 128:T0 + (i + 1) * 128, hf * 2048:(hf + 1) * 2048]
                        row = (h * 4 + i) * 2 + hf
                        dma("sp", "out%d" % i, dref(dst, DY, row, row + 1), OT(i, slice(hf * 2048, (hf + 1) * 2048)))
                yield

        EPSV = pv(F32, (1,))

        dbgcol = [0]

        def dump(ref, ncols):
            if P.dry:
                return
            c0 = dbgcol[0]
            dbgcol[0] += ncols
            dma("pool", "dbg", dref(dbg_d[:, c0:c0 + ncols], DDBG, c0, c0 + ncols), ref)

        class _Stop(Exception):
            pass

        def chk(tag, items):
            if KSTOP == tag:
                for (r, n) in items:
                    dump(r, n)
                raise _Stop()

        def whole():
            try:
                whole_()
            except _Stop:
                pass

        def whole_():
            rot["acc"] = 0
            rot["half"] = 0
            ring["next_get"] = 0
            del pend_stats[:]
            if not P.dry:
                memset(EPSV(), EPS)
                setup()
            if KSTOP == "setup":
                dump(MODT(), 192); dump(A1(), 32); dump(A2(), 32); dump(WMT(0), 128); dump(WMT(15), 128); dump(VECT(), 224)
                return
            for h in range(2):
                pass
                if KSTOP == "p0":
                    dump(H1T(0), 512); dump(H1T(31), 512); dump(HALO(0), 16); dump(HALO(31), 16)
                    return
                phase1(h)
                if KSTOP == "p1":
                    dump(MIXB(0), 512); dump(MIXB(15), 512); dump(GVB(0, slice(0, 512)), 512)
                    return
                phase2(h)
                if KSTOP == "p2":
                    dump(MIXB(0), 512); dump(MIXB(15), 512)
                    return
                phase3(h)
                if KSTOP == "p3":
                    dump(YB(0), 512); dump(YB(15), 512); dump(POOLED(0), 512); dump(POOLED(15), 512)
                    return
                phase4(h)
                if KSTOP == "p4":
                    dump(MERGED(0), 512); dump(MERGED(31), 512)
                    return
                phase5(h)
                if not P.dry:
                    phase6(h)
                if KSTOP == "p6":
                    dump(X2T(0), 512); dump(X2T(31), 512); dump(H2T(0), 512); dump(H2T(31), 512)
                    return
                phase7(h)
                if KSTOP == "p7":
                    dump(AT(0), 512); dump(AT(127), 512)
                    return
                phase8(h)
                if not P.dry:
                    it9 = phase9(h, None)
                    it0 = phase0(h + 1) if h == 0 else iter(())
                    for step, _ in enumerate(it9):
                        if step % 6 == 5:
                            next(it0, None)
                    for _ in it0:
                        pass

        P.dry = True
        whole()
        P.dry = False
        whole()
        fence = P.add("sp", None, reads=[dref(None, DY, 0, 16), dref(None, DDBG, 0, 8192)])

        streams = P.streams()
        with contextlib.ExitStack() as es2:
            sems = {}
            for s in streams:
                sems[s] = es2.enter_context(nc.semaphore("s_" + s.replace(":", "_")))
            block = es2.enter_context(nc.Block())
            P.emit(nc, block, sems)
    return nc


def _tile(w, nk, ncols):
    kk, nn = w.shape
    nkb, ncb = kk // (nk * 128), nn // ncols
    t = w.reshape(nkb, nk, 128, ncb, ncols).transpose(0, 3, 2, 1, 4)
    return np.ascontiguousarray(t).reshape(nkb * ncb * 128, nk * ncols)


def _host_inputs(inputs):
    f = np.float32
    x = np.asarray(inputs["x"], f)[0]
    g = lambda k: np.asarray(inputs[k], f)
    vecs = np.concatenate([
        g("c")[0].reshape(32, 128), g("norm1_g")[0].reshape(32, 128), g("norm2_g")[0].reshape(32, 128),
        g("norm_f_g").reshape(32, 128), g("b_pool")[0].reshape(16, 128), g("pool_scale")[0].reshape(16, 128),
        g("b_gate")[0, 0].reshape(32, 128), g("b_gate")[0, 1].reshape(32, 128)], axis=0)
    common = {
        "vecs": np.ascontiguousarray(vecs),
        "bada": np.ascontiguousarray(g("b_ada")[0].reshape(192, 128)),
        "identf": np.eye(128, dtype=f),
        "tril": np.tril(np.ones((128, 128), f)),
        "lng": np.ascontiguousarray(np.broadcast_to(g("ln_v_g")[0][None, :], (128, 2048))),
        "lnb": np.ascontiguousarray(np.broadcast_to(g("ln_v_b")[0][None, :], (128, 2048))),
        "bsp": np.ascontiguousarray(np.broadcast_to(g("b_spatial")[0].reshape(1, 2048), (128, 2048))),
        "w_ada": g("w_ada")[0], "w_sp": g("w_spatial")[0],
        "w_pool": np.ascontiguousarray(g("w_pool")[0].reshape(2048, 512)),
        "wt_in": _tile(g("w_in")[0], 16, 256), "wt_v8": _tile(g("w_in")[0][:, 2048:4096], 8, 512),
        "wt_upa": _tile(g("w_up_a")[0], 16, 256), "wt_upb": _tile(g("w_up_b")[0], 16, 256),
        "wt_out": _tile(g("w_out")[0], 16, 256),
        "wt_ff1": _tile(g("w_ff1")[0], 16, 256), "wt_ff2": _tile(g("w_ff2")[0], 16, 256),
    }
    wins = np.array([2, 4, 8, 16], f)
    in_maps = []
    for c in range(NCORES):
        t0 = c * TOK
        m = dict(common)
        m["x"] = np.ascontiguousarray(x[t0:t0 + TOK])
        xh = np.zeros((32, D), f)
        hm = np.ones((128, 2), f)
        invc = np.zeros((2, 128, 4, TP), f)
        for h in range(2):
            s = t0 + h * TP
            if s >= 16:
                xh[h * 16:(h + 1) * 16] = x[s - 16:s]
            else:
                hm[:, h] = 0.0
            t = np.arange(s, s + TP, dtype=f)
            cnt = np.minimum(t[None, :] + 1.0, wins[:, None])
            invc[h] = (1.0 / cnt)[None]
        m["xh"] = xh
        m["hmask"] = hm
        m["invc"] = np.ascontiguousarray(invc.reshape(2 * 128, 4 * TP))
        in_maps.append(m)
    return in_maps


_NC_CACHE = {}


def kernel(**inputs):
    in_maps = _host_inputs(inputs)
    if "nc" not in _NC_CACHE:
        _NC_CACHE["nc"] = build_program()
    nc = _NC_CACHE["nc"]
    res = run_bass_kernel_spmd(nc, in_maps, core_ids=list(range(NCORES)))
    out = np.concatenate([np.asarray(r["y"], np.float32) for r in res.results], axis=0)
    return out.reshape(1, NCORES * TOK, D)
```

```python
import numpy as np
import concourse.bass as bass
import concourse.mybir as mybir
from concourse.bass_utils import run_bass_kernel_spmd

dt = mybir.dt
F32, BF16, F32R = dt.float32, dt.bfloat16, dt.float32r
AF = mybir.ActivationFunctionType
ALU = mybir.AluOpType
ES = {F32: 4, BF16: 2, F32R: 4}
K = 1024

NCORES = 8
TOK = 1024
TP = 512
D = 4096
DFF = 16384
NSLOT = 4
SLOT_B = 8 * K
EPS = 1e-6
GELU_C = 1.5957691216057308
USE_ACT_GELU = False
import os
KSTOP = os.environ.get("KSTOP", "")


class Arena:
    def __init__(self, name, t, base_dt):
        self.name, self.t, self.base_dt = name, t, base_dt


class Ref:
    __slots__ = ("ap", "arena", "lo", "hi")

    def __init__(self, ap, arena, lo, hi):
        self.ap, self.arena, self.lo, self.hi = ap, arena, lo, hi


class View:
    def __init__(self, arena, off, dtype, shape):
        self.arena, self.off, self.dtype, self.shape = arena, off, dtype, tuple(shape)
        es = ES[dtype]
        n = int(np.prod(shape))
        bes = ES[arena.base_dt]
        assert off % bes == 0 and (n * es) % bes == 0
        base = arena.t[:, off // bes:(off + n * es) // bes]
        ap = base.bitcast(dtype) if dtype != arena.base_dt else base
        if len(shape) == 2:
            ap = ap.rearrange("p (a b) -> p a b", b=shape[1])
        elif len(shape) == 3:
            ap = ap.rearrange("p (a b c) -> p a b c", b=shape[1], c=shape[2])
        self.full = ap
        st = []
        acc = 1
        for d_ in reversed(self.shape):
            st.append(acc)
            acc *= d_
        self.strides = tuple(reversed(st))

    def __call__(self, *idx, p=None):
        idx = tuple(idx) + (slice(None),) * (len(self.shape) - len(idx))
        lo = hi = 0
        for i, dim, st in zip(idx, self.shape, self.strides):
            if isinstance(i, int):
                assert 0 <= i < dim, (i, dim)
                lo += i * st
                hi += i * st
            else:
                a, b, _ = i.indices(dim)
                assert b > a
                lo += a * st
                hi += (b - 1) * st
        es = ES[self.dtype]
        psl = slice(None) if p is None else slice(p[0], p[1])
        ap = self.full[(psl,) + idx]
        return Ref(ap, self.arena, self.off + lo * es, self.off + (hi + 1) * es)


class DramArena:
    def __init__(self, name):
        self.name = name


def dref(ap, arena, lo=0, hi=1):
    return Ref(ap, arena, lo, hi)


class Op:
    __slots__ = ("eng", "stream", "fn", "deps", "needs_inc", "ticket", "is_dma", "idx")


class Rec:
    __slots__ = ("lo", "hi", "writer", "readers")


class Prog:
    ENGS = ("pe", "act", "dve", "pool", "sp")

    def __init__(self):
        self.ops = []
        self.recs = {}
        self.last_chan = {}
        self.dry = False

    def add(self, eng, fn, reads=(), writes=(), chan=None):
        if self.dry:
            return None
        op = Op()
        op.eng, op.fn, op.deps, op.needs_inc, op.ticket = eng, fn, {}, False, 0
        op.is_dma = chan is not None
        op.stream = ("dma:" + chan) if chan is not None else eng
        op.idx = len(self.ops)
        if chan is not None:
            prev = self.last_chan.get(chan)
            if prev is not None:
                op.deps[prev.idx] = prev
            self.last_chan[chan] = op
        for r in reads:
            self._access(op, r, False)
        for w in writes:
            self._access(op, w, True)
        self.ops.append(op)
        return op

    def _dep(self, op, other):
        if other is op:
            return
        if other.stream == "pe" and op.stream == "pe":
            return
        op.deps[other.idx] = other

    def _access(self, op, ref, is_write):
        name = ref.arena.name
        if name == "psum":
            ref = Ref(ref.ap, ref.arena, ref.lo // 2048 * 2048, (ref.hi + 2047) // 2048 * 2048)
            is_write = True
        recs = self.recs.get(name, [])
        new = []
        for r in recs:
            if r.hi <= ref.lo or r.lo >= ref.hi:
                new.append(r)
                continue
            if r.writer is not None:
                self._dep(op, r.writer)
            if is_write:
                for rd in r.readers.values():
                    self._dep(op, rd)
                if ref.lo <= r.lo and r.hi <= ref.hi:
                    continue
                new.append(r)
            else:
                r.readers[op.stream] = op
                new.append(r)
        if is_write:
            r = Rec()
            r.lo, r.hi, r.writer, r.readers = ref.lo, ref.hi, op, {}
            new.append(r)
        self.recs[name] = new

    def emit(self, nc, block, sems):
        for op in self.ops:
            for d_ in op.deps.values():
                d_.needs_inc = True
        cnt = {}
        for op in self.ops:
            if op.is_dma:
                cnt[op.stream] = cnt.get(op.stream, 0) + 16
                op.ticket = cnt[op.stream]
            elif op.needs_inc:
                cnt[op.stream] = cnt.get(op.stream, 0) + 1
                op.ticket = cnt[op.stream]
        self.max_counts = cnt

        def run(engname):
            def body(e):
                waited = {}
                for op in self.ops:
                    if op.eng != engname:
                        continue
                    for d_ in op.deps.values():
                        if waited.get(d_.stream, 0) < d_.ticket:
                            e.wait_ge(sems[d_.stream], d_.ticket)
                            waited[d_.stream] = d_.ticket
                    if op.fn is None:
                        continue
                    ins = op.fn(e)
                    if op.is_dma:
                        ins.then_inc(sems[op.stream], 16)
                    elif op.needs_inc:
                        ins.then_inc(sems[op.stream], 1)
            return body

        block.tensor(run("pe"))
        block.scalar(run("act"))
        block.vector(run("dve"))
        block.gpsimd(run("pool"))
        block.sync(run("sp"))

    def streams(self):
        s = []
        for op in self.ops:
            if op.stream not in s:
                s.append(op.stream)
        return s


def build_program():
    nc = bass.Bass("TRN2", target_bir_lowering=False)

    def din(name, shape):
        return nc.dram_tensor(name, list(shape), F32, kind="ExternalInput").ap()

    x_d = din("x", [TOK, D])
    xh_d = din("xh", [32, D])
    hmask_d = din("hmask", [128, 2])
    invc_d = din("invc", [2 * 128, 4 * TP])
    vecs_d = din("vecs", [224, 128])
    bada_d = din("bada", [192, 128])
    identf_d = din("identf", [128, 128])
    tril_d = din("tril", [128, 128])
    lng_d = din("lng", [128, 2048])
    lnb_d = din("lnb", [128, 2048])
    bsp_d = din("bsp", [128, 2048])
    wada_d = din("w_ada", [D, 6 * D])
    wt_in_d = din("wt_in", [2 * 56 * 128, 4096])
    wt_v8_d = din("wt_v8", [16 * 128, 4096])
    wsp_d = din("w_sp", [16, 128, 128])
    wpool_d = din("w_pool", [2048, 512])
    wt_upa_d = din("wt_upa", [16 * 128, 4096])
    wt_upb_d = din("wt_upb", [16 * 128, 4096])
    wt_out_d = din("wt_out", [32 * 128, 4096])
    wt_ff1_d = din("wt_ff1", [128 * 128, 4096])
    wt_ff2_d = din("wt_ff2", [128 * 128, 4096])
    TILED = {"in": (wt_in_d, 56), "v8": (wt_v8_d, 4), "upa": (wt_upa_d, 16), "upb": (wt_upb_d, 16),
             "out": (wt_out_d, 16), "ff1": (wt_ff1_d, 64), "ff2": (wt_ff2_d, 16)}
    win_d, wupa_d, wupb_d, wout_d, wff1_d, wff2_d = "in", "upa", "upb", "out", "ff1", "ff2"
    y_d = nc.dram_tensor("y", [TOK, D], F32, kind="ExternalOutput").ap()
    dbg_d = nc.dram_tensor("dbg", [128, 8192], F32, kind="ExternalOutput").ap() if KSTOP else None
    DDBG = DramArena("dbg")
    x2d = nc.dram_tensor("x2d", [2 * 32 * 128, TP], F32).ap()
    modd = nc.dram_tensor("modd", [192, 128], F32).ap()
    DX2 = DramArena("x2d")
    DMOD = DramArena("modd")
    DY = DramArena("y")
    DIN = DramArena("in")

    P = Prog()
    PERSIST_B = 10 * K
    SCR_B = 2 * K
    BIG_B = 160 * K

    import contextlib
    with contextlib.ExitStack() as es:
        persist_t = es.enter_context(nc.sbuf_tensor("persist", [128, PERSIST_B // 4], F32))
        scr_t = es.enter_context(nc.sbuf_tensor("scr", [128, SCR_B // 2], BF16))
        ring_t = es.enter_context(nc.sbuf_tensor("ring", [128, NSLOT * SLOT_B // 2], BF16))
        big_t = es.enter_context(nc.sbuf_tensor("big", [128, BIG_B // 2], BF16))
        psum_t = es.enter_context(nc.psum_tensor("ps", [128, 4096], F32))
        APER = Arena("persist", persist_t, F32)
        ASCR = Arena("scr", scr_t, BF16)
        ARING = Arena("ring", ring_t, BF16)
        ABIG = Arena("big", big_t, BF16)
        APS = Arena("psum", psum_t, F32)

        off = [0]

        def pv(dtype, shape):
            n = int(np.prod(shape)) * ES[dtype]
            n = (n + 31) // 32 * 32
            v = View(APER, off[0], dtype, shape)
            off[0] += n
            assert off[0] <= PERSIST_B
            return v

        IDF = pv(F32, (128,))
        IDB = pv(BF16, (128,))
        ONESB = pv(BF16, (128,))
        VECT = pv(F32, (224,))
        BADAT = pv(F32, (192,))
        MODT = pv(F32, (192,))
        A1 = pv(F32, (32,))
        A2 = pv(F32, (32,))
        CACT = pv(F32, (32,))
        WMT = pv(BF16, (16, 128))
        HMASK = pv(F32, (2,))
        SS = pv(F32, (1,))
        SS2 = pv(F32, (1,))
        RSTD1 = pv(F32, (1,))
        MV = pv(F32, (2,))
        VS = pv(F32, (1,))
        VR = pv(F32, (1,))
        NB = pv(F32, (1,))
        STATS = pv(F32, (4, 4, 6))
        C_G1, C_G2, C_GF, C_BP, C_PS, C_BGA, C_BGB = 32, 64, 96, 128, 144, 160, 192
        M_SH1, M_SC1, M_GT1, M_SH2, M_SC2, M_GT2 = 0, 32, 64, 96, 128, 160

        RING = [View(ARING, s * SLOT_B, BF16, (16, 256)) for s in range(NSLOT)]
        RINGW = [View(ARING, s * SLOT_B, BF16, (8, 512)) for s in range(NSLOT)]
        RL = [View(ASCR, i * K, BF16, (TP,)) for i in range(2)]

        def bv(offk, dtype, shape):
            return View(ABIG, int(offk * K), dtype, shape)

        H1T = bv(0, BF16, (32, TP))
        HALO = bv(32, BF16, (32, 16))
        MIXB = bv(33, BF16, (16, TP))
        YB = bv(49, BF16, (16, TP))
        MERGED = bv(65, BF16, (32, TP))
        XT = [bv(97, F32, (D,)), bv(113, F32, (D,))]
        XN = bv(129, BF16, (D,))
        GVB = bv(97, BF16, (4, 2048))
        ZT = bv(113, F32, (2048,))
        LNG = bv(121, F32, (2048,))
        LNB = bv(129, F32, (2048,))
        BSP = bv(137, F32, (16, 128))
        GT = [bv(145, F32, (256,)), bv(146, F32, (256,)), bv(147, F32, (TP,)), bv(149, F32, (TP,))]
        WPOOL = bv(97, BF16, (16, 512))
        POOLED = bv(113, BF16, (16, TP))
        INVC = bv(129, F32, (4, TP))
        PT = [bv(137, F32, (528,)), bv(137 + 2.125, F32, (528,))]
        SA = bv(137 + 4.25, F32, (528,))
        SB = bv(137 + 6.375, F32, (528,))
        SGA = bv(97, F32, (2, TP))
        SGB = bv(101, F32, (2, TP))
        MP = bv(105, F32, (2, TP))
        X2T = bv(0, F32, (32, TP))
        XC = [bv(97, F32, (4, 256)), bv(101, F32, (4, 256))]
        TT5 = [bv(105, F32, (TP,)), bv(107, F32, (TP,))]
        SQ5 = [bv(109, BF16, (TP,)), bv(110, BF16, (TP,)), bv(117, BF16, (TP,)), bv(118, BF16, (TP,))]
        RSTD2 = bv(111, F32, (TP,))
        TMP6 = [bv(113, F32, (TP,)), bv(115, F32, (TP,)), bv(119, F32, (TP,)), bv(121, F32, (TP,))]
        H2T = bv(128, BF16, (32, TP))
        AT = bv(0, BF16, (128, TP))
        TT8 = [bv(128, F32, (TP,)), bv(130, F32, (TP,))]
        XS = [bv(132, F32, (TP,)), bv(134, F32, (TP,))]
        X3S = [bv(136, F32, (TP,)), bv(138, F32, (TP,))]
        SQ8 = [bv(140, BF16, (TP,)), bv(141, BF16, (TP,)), bv(144, BF16, (TP,)), bv(145, BF16, (TP,))]
        RSTD3 = bv(142, F32, (TP,))
        OT = bv(33, F32, (4, D))
        XS9 = [bv(137 + 2 * i, F32, (TP,)) for i in range(6)]
        OS9 = [bv(149 + 2 * i, F32, (TP,)) for i in range(4)]
        RSTD9 = bv(157, F32, (TP,))
        CBC = bv(0, F32, (32, 128))
        CBCB = bv(16, BF16, (32, 128))
        WA = [bv(52 + 16 * i, F32, (D,)) for i in range(3)]
        WAC = [bv(100 + 8 * i, BF16, (D,)) for i in range(3)]
        WAB = [bv(124 + 8 * i, BF16, (D,)) for i in range(3)]
        MODROW = bv(24, F32, (D,))
        WSP = bv(40, F32, (16, 128))
        RA = [bv(48 + 0.5 * i, F32, (128,)) for i in range(4)]
        TRIL = bv(50, F32, (128,))
        ONESF = bv(50.5, F32, (128,))

        def bank(b, dtype=F32, shape=(512,)):
            return View(APS, b * 2048, dtype, shape)

        def hbank(hb, dtype=F32, shape=(256,)):
            return View(APS, hb * 1024, dtype, shape)

        rot = {"acc": 0, "half": 0}

        def acc_bank():
            b = rot["acc"] % 6
            rot["acc"] += 1
            return b

        def acc_half():
            b = rot["half"] % 12
            rot["half"] += 1
            return b

        def dma(eng, chan, out, in_):
            P.add(eng, lambda e, o=out.ap, i=in_.ap: e.dma_start(out=o, in_=i),
                  reads=[in_], writes=[out], chan=chan)

        def act(out, in_, func, scale=1.0, bias=0.0, accum=None, extra_reads=()):
            def fn(e, o=out.ap, i=in_.ap):
                kw = {}
                if accum is not None:
                    kw["accum_out"] = accum.ap
                sc = scale.ap if isinstance(scale, Ref) else float(scale)
                bi = bias.ap if isinstance(bias, Ref) else float(bias)
                return e.activation(o, i, func, bias=bi, scale=sc, **kw)
            rd = [in_] + list(extra_reads)
            if isinstance(scale, Ref):
                rd.append(scale)
            if isinstance(bias, Ref):
                rd.append(bias)
            wr = [out] + ([accum] if accum is not None else [])
            P.add("act", fn, reads=rd, writes=wr)

        def ts(out, in0, s1, s2, op0, op1=None, eng="dve"):
            def fn(e, o=out.ap, i=in0.ap):
                a = s1.ap if isinstance(s1, Ref) else s1
                b = s2.ap if isinstance(s2, Ref) else s2
                if op1 is None:
                    return e.tensor_scalar(o, i, a, None, op0)
                return e.tensor_scalar(o, i, a, b, op0, op1)
            rd = [in0] + [s for s in (s1, s2) if isinstance(s, Ref)]
            P.add(eng, fn, reads=rd, writes=[out])

        def tt(out, in0, in1, op, eng="dve"):
            P.add(eng, lambda e, o=out.ap, a=in0.ap, b=in1.ap: e.tensor_tensor(o, a, b, op),
                  reads=[in0, in1], writes=[out])

        def stt(out, in0, scalar, in1, op0, op1):
            def fn(e, o=out.ap, a=in0.ap, b=in1.ap):
                s = scalar.ap if isinstance(scalar, Ref) else scalar
                return e.scalar_tensor_tensor(o, a, s, b, op0, op1)
            rd = [in0, in1] + ([scalar] if isinstance(scalar, Ref) else [])
            P.add("dve", fn, reads=rd, writes=[out])

        def recip(out, in_):
            P.add("dve", lambda e, o=out.ap, i=in_.ap: e.reciprocal(o, i), reads=[in_], writes=[out])

        def copy(out, in_, eng="dve"):
            P.add(eng, lambda e, o=out.ap, i=in_.ap: e.tensor_copy(o, i), reads=[in_], writes=[out])

        def memset(out, val, eng="dve"):
            P.add(eng, lambda e, o=out.ap: e.memset(o, val), writes=[out])

        def mm_group(out, pairs, start, stop, reads):
            n = len(pairs)

            def fn(e, o=out.ap):
                ins = None
                for q, (l, r) in enumerate(pairs):
                    ins = e.matmul(o, l, r, start=(start and q == 0), stop=(stop and q == n - 1))
                return ins
            P.add("pe", fn, reads=reads, writes=[out])

        def transposes(items, reads, writes):
            def fn(e):
                ins = None
                for (o, i, idn) in items:
                    ins = e.transpose(o, i, idn)
                return ins
            P.add("pe", fn, reads=reads, writes=writes)

        def gelu_evac(src, dst, tA):
            if USE_ACT_GELU:
                act(dst, src, AF.Gelu_apprx_tanh)
                return
            act(tA, src, AF.Square)
            ts(tA, tA, 0.044715, 1.0, ALU.mult, ALU.add)
            tt(tA, tA, src, ALU.mult)
            act(tA, tA, AF.Sigmoid, scale=GELU_C)
            tt(dst, tA, src, ALU.mult)

        ring = {"specs": [], "next_get": 0, "next_load": 0}

        def ring_load(n):
            wd, r0, c0, nk, ncols = ring["specs"][n]
            s = n % NSLOT
            if ncols == 512:
                tap, ncb = TILED["v8"]
                blk = (r0 // 1024) * ncb + (c0 - 2048) // 512
            else:
                tap, ncb = TILED[wd]
                blk = (r0 // 2048) * ncb + c0 // 256
            src = tap[blk * 128:(blk + 1) * 128, 0:nk * ncols].rearrange("p (k c) -> p k c", c=ncols)
            dst = RING[s] if ncols == 256 else RINGW[s]
            dma("pool", "ring%d" % s, dst(slice(0, nk)), dref(src, DIN))

        def ring_get(wd, r0, c0, nk=16, ncols=256):
            n = ring["next_get"]
            ring["next_get"] += 1
            ret = RING[n % NSLOT] if ncols == 256 else RINGW[n % NSLOT]
            if P.dry:
                ring["specs"].append((wd, r0, c0, nk, ncols))
                return ret
            if n == 0:
                for m in range(min(NSLOT, len(ring["specs"]))):
                    ring_load(m)
                ring["next_load"] = NSLOT
            assert ring["specs"][n][1:] == (r0, c0, nk, ncols)
            return ret

        def ring_release():
            if P.dry:
                return
            m = ring["next_load"]
            if m < len(ring["specs"]):
                ring_load(m)
            ring["next_load"] += 1

        def load_rows_T(src_d, nrows, dstT, col0, arena_d=DIN):
            done = 0
            q = 0
            while done < nrows:
                n = min(128, nrows - done)
                ra = RA[q % 4]
                dma("sp", "ra%d" % (q % 4), ra(p=(0, n)), dref(src_d[done:done + n, :], arena_d))
                b = 6 + (q % 2)
                pb = bank(b, F32, (128,))
                transposes([(pb(slice(0, n)).ap, ra(p=(0, n)).ap, IDF(slice(0, n), p=(0, n)).ap)],
                           reads=[ra(p=(0, n)), IDF()], writes=[pb(slice(0, n))])
                copy(dstT(slice(col0 + done, col0 + done + n)), pb(slice(0, n)))
                done += n
                q += 1

        def setup():
            dma("sp", "c0", IDF(), dref(identf_d, DIN))
            dma("sp", "c1", TRIL(), dref(tril_d, DIN))
            dma("sp", "c2", HMASK(), dref(hmask_d, DIN))
            memset(ONESF(), 1.0)
            memset(ONESB(), 1.0)
            copy(IDB(), IDF())
            load_rows_T(vecs_d, 224, VECT, 0)
            load_rows_T(bada_d, 192, BADAT, 0)
            act(CACT(), VECT(slice(0, 32)), AF.Sigmoid)
            tt(CACT(), CACT(), VECT(slice(0, 32)), ALU.mult)
            for kc in range(32):
                ts(CBC(kc), ONESF(), CACT(slice(kc, kc + 1)), None, ALU.mult)
            copy(CBCB(), CBC())
            qf = 0
            qb = 0
            for g in range(6):
                for kc in range(32):
                    src = dref(wada_d[kc * 128:(kc + 1) * 128, g * D:(g + 1) * D], DIN)
                    if kc % 2 == 0:
                        wf = WA[qf % 3]
                        dma("sp", "wa%d" % (qf % 3), wf(), src)
                        wa = WAC[qf % 3]
                        if qf % 2 == 0:
                            act(wa(), wf(), AF.Copy)
                        else:
                            copy(wa(), wf())
                        qf += 1
                        lhs = CBCB(kc)
                    else:
                        wa = WAB[qb % 3]
                        dma("pool", "wab%d" % (qb % 3), wa(), src)
                        qb += 1
                        lhs = CBCB(kc)
                    for b in range(8):
                        pb = bank(b)
                        mm_group(pb(), [(lhs.ap, wa(slice(b * 512, (b + 1) * 512)).ap)],
                                 start=(kc == 0), stop=(kc == 31), reads=[lhs, wa()])
                for b in range(8):
                    act(MODROW(slice(b * 512, (b + 1) * 512), p=(0, 1)), bank(b)(p=(0, 1)), AF.Copy)
                dst = modd[g * 32:(g + 1) * 32, :].rearrange("(o r) c -> o (r c)", o=1)
                dma("sp", "modst", dref(dst, DMOD, g, g + 1), MODROW(p=(0, 1)))
            done = 0
            for q2, n in enumerate((128, 64)):
                ra = RA[q2]
                dma("sp", "ra%d" % q2, ra(p=(0, n)), dref(modd[done:done + n, :], DMOD, 0, 6))
                pb = bank(6 + q2, F32, (128,))
                transposes([(pb(slice(0, n)).ap, ra(p=(0, n)).ap, IDF(slice(0, n), p=(0, n)).ap)],
                           reads=[ra(p=(0, n)), IDF()], writes=[pb(slice(0, n))])
                tt(MODT(slice(done, done + n)), pb(slice(0, n)), BADAT(slice(done, done + n)), ALU.add)
                done += n
            stt(A1(), MODT(slice(M_SC1, M_SC1 + 32)), 1.0, VECT(slice(C_G1, C_G1 + 32)), ALU.add, ALU.mult)
            stt(A2(), MODT(slice(M_SC2, M_SC2 + 32)), 1.0, VECT(slice(C_G2, C_G2 + 32)), ALU.add, ALU.mult)
            dma("sp", "c3", WSP(), dref(wsp_d.rearrange("h t s -> t h s"), DIN))
            for hd in range(16):
                tt(WSP(hd), WSP(hd), TRIL(), ALU.mult)
            for g in range(4):
                pb = bank(6 + (g % 2), F32, (4, 128))
                transposes([(pb(q).ap, WSP(g * 4 + q).ap, IDF().ap) for q in range(4)],
                           reads=[WSP(slice(g * 4, g * 4 + 4)), IDF()], writes=[pb()])
                copy(WMT(slice(g * 4, g * 4 + 4)), pb())

        def phase0(h):
            T0 = h * TP
            for i in range(5):
                halo = (i == 4)
                n = 16 if halo else 128
                pp = (0, n)
                xt = XT[i % 2]
                src = xh_d[h * 16:(h + 1) * 16, :] if halo else x_d[T0 + i * 128:T0 + (i + 1) * 128, :]
                dma("sp", "xt%d" % (i % 2), xt(p=pp), dref(src, DIN))
                act(XN(p=pp), xt(p=pp), AF.Square, accum=SS(p=pp))
                act(SS2(p=pp), SS(p=pp), AF.Sqrt, scale=1.0 / D, bias=EPSV(p=pp))
                recip(RSTD1(p=pp), SS2(p=pp))
                act(XN(p=pp), xt(p=pp), AF.Identity, scale=RSTD1(p=pp))
                for g in range(4):
                    pb = bank(6 + (g % 2), BF16, (8, 128))
                    items = []
                    for q in range(8):
                        kc = g * 8 + q
                        items.append((pb(q, slice(0, n)).ap, XN(slice(kc * 128, (kc + 1) * 128), p=pp).ap,
                                      IDB(slice(0, n), p=pp).ap))
                    transposes(items, reads=[XN(p=pp), IDB()], writes=[pb()])
                    for q in range(8):
                        kc = g * 8 + q
                        dst = HALO(kc) if halo else H1T(kc, slice(i * 128, (i + 1) * 128))
                        ts(dst, pb(q, slice(0, n)), A1(slice(kc, kc + 1)), MODT(slice(M_SH1 + kc, M_SH1 + kc + 1)),
                           ALU.mult, ALU.add)
                yield

        def phase1(h):
            dma("sp", "lng", LNG(), dref(lng_d, DIN))
            dma("sp", "lnb", LNB(), dref(lnb_d, DIN))
            dma("sp", "bsp", BSP(), dref(bsp_d.rearrange("p (a b) -> p a b", b=128), DIN))
            for cb in range(4):
                c0 = 2048 + cb * 512
                pbs = [bank(acc_bank()) for _ in range(4)]
                for kb in range(4):
                    slot = ring_get(win_d, kb * 1024, c0, 8, 512)
                    for i in range(4):
                        pairs = [(H1T(kb * 8 + k, slice(i * 128, (i + 1) * 128)).ap, slot(k).ap) for k in range(8)]
                        mm_group(pbs[i](), pairs, start=(kb == 0), stop=(kb == 3),
                                 reads=[slot(), H1T(slice(kb * 8, kb * 8 + 8))])
                    ring_release()
                for i in range(4):
                    gdst = GVB(i, slice(cb * 512, (cb + 1) * 512))
                    gelu_evac(pbs[i](), gdst, GT[2 + (i % 2)]())
                    P.add("dve", lambda e, o=STATS(i, cb).ap, a=gdst.ap: e.bn_stats(o, a),
                          reads=[gdst], writes=[STATS(i, cb)])
            chk("p1a", [(GVB(0, slice(0, 512)), 512), (STATS(0), 24)])
            for i in range(4):
                P.add("dve", lambda e, o=MV().ap, a=STATS(i).ap: e.bn_aggr(o, a), reads=[STATS(i)], writes=[MV()])
                act(VS(), MV(slice(1, 2)), AF.Sqrt, scale=1.0, bias=EPSV())
                recip(VR(), VS())
                stt(NB(), MV(slice(0, 1)), -1.0, VR(), ALU.mult, ALU.mult)
                act(ZT(), GVB(i), AF.Identity, scale=VR(), bias=NB())
                tt(ZT(), ZT(), LNG(), ALU.mult)
                tt(GVB(i), ZT(), LNB(), ALU.add)
                if i == 0:
                    chk("p1b", [(GVB(0, slice(0, 512)), 512), (MV(), 2)])
                for g in range(4):
                    pb = bank(6 + (g % 2), F32, (4, 128))
                    for q in range(4):
                        hd = g * 4 + q
                        mm_group(pb(q), [(GVB(i, slice(hd * 128, (hd + 1) * 128)).ap, WMT(hd).ap)], True, True,
                                 reads=[GVB(i), WMT(hd)])
                    tt(MIXB(slice(g * 4, g * 4 + 4), slice(i * 128, (i + 1) * 128)), pb(),
                       BSP(slice(g * 4, g * 4 + 4)), ALU.add)

        def fm_proj(wd, c0, nkb, act_view, nchunks_rows=16):
            banks = [bank(acc_bank()) for _ in range(2)]
            for kb in range(nkb):
                slot = ring_get(wd, kb * 2048, c0, nchunks_rows)
                for jl in range(2):
                    pairs = [(slot(k, slice(jl * 128, (jl + 1) * 128)).ap, act_view(kb * 16 + k).ap)
                             for k in range(nchunks_rows)]
                    mm_group(banks[jl](), pairs, start=(kb == 0), stop=(kb == nkb - 1),
                             reads=[slot(), act_view(slice(kb * 16, kb * 16 + nchunks_rows))])
                ring_release()
            return banks

        def phase2(h):
            for cb in range(8):
                banks = fm_proj(win_d, cb * 256, 2, H1T)
                for jl in range(2):
                    j = cb * 2 + jl
                    tmp = GT[2 + jl]
                    t2 = TT5[jl]
                    gelu_evac(banks[jl](), t2(), tmp())
                    tt(MIXB(j), t2(), MIXB(j), ALU.mult)

        def phase3(h):
            dma("pool", "wpool", WPOOL(), dref(wpool_d.rearrange("(a p) d -> p a d", p=128), DIN))
            dma("sp", "invc", INVC(), dref(invc_d[h * 128:(h + 1) * 128, :].rearrange("p (a b) -> p a b", b=TP), DIN))
            for cb in range(8):
                c0 = 4096 + cb * 256
                pbs = [bank(acc_bank()) for _ in range(2)]
                hbs = [bank(6 + jl, F32, (16,)) for jl in range(2)]
                for kb in range(2):
                    slot = ring_get(win_d, kb * 2048, c0)
                    for jl in range(2):
                        pairs = [(slot(k, slice(jl * 128, (jl + 1) * 128)).ap, H1T(kb * 16 + k).ap) for k in range(16)]
                        mm_group(pbs[jl](), pairs, start=(kb == 0), stop=(kb == 1),
                                 reads=[slot(), H1T(slice(kb * 16, kb * 16 + 16))])
                        pairs = [(slot(k, slice(jl * 128, (jl + 1) * 128)).ap, HALO(kb * 16 + k).ap) for k in range(16)]
                        mm_group(hbs[jl](), pairs, start=(kb == 0), stop=(kb == 1),
                                 reads=[slot(), HALO(slice(kb * 16, kb * 16 + 16))])
                    ring_release()
                for jl in range(2):
                    j = cb * 2 + jl
                    g = j // 4
                    pb = pbs[jl]
                    hb = hbs[jl]
                    pt = PT[j % 2]
                    act(pt(slice(16, 528)), pb(), AF.Copy)
                    ts(pt(slice(0, 16)), hb(), HMASK(slice(h, h + 1)), None, ALU.mult)
                    tt(SA(slice(1, 528)), pt(slice(1, 528)), pt(slice(0, 527)), ALU.add)
                    cur = SA
                    if g >= 1:
                        tt(SB(slice(3, 528)), SA(slice(3, 528)), SA(slice(1, 526)), ALU.add)
                        cur = SB
                    if g >= 2:
                        tt(SA(slice(7, 528)), SB(slice(7, 528)), SB(slice(3, 524)), ALU.add)
                        cur = SA
                    if g >= 3:
                        tt(SB(slice(15, 528)), SA(slice(15, 528)), SA(slice(7, 520)), ALU.add)
                        cur = SB
                    oth = SB if cur is SA else SA
                    tt(oth(slice(16, 528)), cur(slice(16, 528)), INVC(g), ALU.mult)
                    tt(POOLED(j), oth(slice(16, 528)), pt(slice(16, 528)), ALU.subtract)
            for g in range(4):
                for dj in range(4):
                    pb = bank(acc_bank())
                    pairs = [(WPOOL(g * 4 + cc, slice(dj * 128, (dj + 1) * 128)).ap, POOLED(g * 4 + cc).ap)
                             for cc in range(4)]
                    mm_group(pb(), pairs, True, True, reads=[WPOOL(), POOLED(slice(g * 4, g * 4 + 4))])
                    j = g * 4 + dj
                    ts(YB(j), pb(), VECT(slice(C_BP + j, C_BP + j + 1)), VECT(slice(C_PS + j, C_PS + j + 1)),
                       ALU.add, ALU.mult)

        def phase4(h):
            for jp in range(16):
                banks = fm_proj(win_d, 6144 + jp * 256, 2, H1T)
                for jl in range(2):
                    j = jp * 2 + jl
                    act(SGA(jl), banks[jl](), AF.Sigmoid, bias=VECT(slice(C_BGA + j, C_BGA + j + 1)))
                banks = fm_proj(win_d, 10240 + jp * 256, 2, H1T)
                for jl in range(2):
                    j = jp * 2 + jl
                    act(SGB(jl), banks[jl](), AF.Sigmoid, bias=VECT(slice(C_BGB + j, C_BGB + j + 1)))
                banks = fm_proj(wupa_d, jp * 256, 1, MIXB)
                for jl in range(2):
                    tt(MP(jl), banks[jl](), SGA(jl), ALU.mult)
                banks = fm_proj(wupb_d, jp * 256, 1, YB)
                for jl in range(2):
                    j = jp * 2 + jl
                    tt(SGB(jl), banks[jl](), SGB(jl), ALU.mult)
                    tt(MERGED(j), MP(jl), SGB(jl), ALU.add)

        pend_stats = []

        def stats_mm(sq, j):
            pend_stats.append((sq, j))

        def flush_stats():
            for (sq, j) in pend_stats:
                mm_group(bank(6)(), [(ONESB().ap, sq.ap)], start=(j == 0), stop=(j == 31), reads=[ONESB(), sq])
            del pend_stats[:]

        def phase5(h):
            T0 = h * TP

            def load_xc(jp):
                src = x_d[T0:T0 + TP, jp * 256:(jp + 1) * 256].rearrange("(i p) c -> p i c", p=128)
                dma("sp", "xc%d" % (jp % 2), XC[jp % 2](), dref(src, DIN))

            load_xc(0)
            for jp in range(16):
                xc = XC[jp % 2]
                xbs = []
                for jl in range(2):
                    xb = bank(acc_bank(), F32, (4, 128))
                    transposes([(xb(i).ap, xc(i, slice(jl * 128, (jl + 1) * 128)).ap, IDF().ap) for i in range(4)],
                               reads=[xc(), IDF()], writes=[xb()])
                    xbs.append(xb)
                if jp + 1 < 16:
                    load_xc(jp + 1)
                banks = fm_proj(wout_d, jp * 256, 2, MERGED)
                flush_stats()
                for jl in range(2):
                    j = jp * 2 + jl
                    act(TT5[jl](), banks[jl](), AF.Identity, scale=MODT(slice(M_GT1 + j, M_GT1 + j + 1)))
                for jl in range(2):
                    j = jp * 2 + jl
                    tt(X2T(j), xbs[jl](), TT5[jl](), ALU.add)
                for jl in range(2):
                    j = jp * 2 + jl
                    sq = SQ5[(jp % 2) * 2 + jl]
                    act(sq(), X2T(j), AF.Square)
                    stats_mm(sq(), j)
                for jl in range(2):
                    j = jp * 2 + jl
                    dst = x2d[(h * 32 + j) * 128:(h * 32 + j + 1) * 128, :]
                    dma("sp", "x2st%d" % (j % 2), dref(dst, DX2, h * 32 + j, h * 32 + j + 1), X2T(j))

        def phase6(h):
            flush_stats()
            act(RSTD2(), bank(6)(), AF.Sqrt, scale=1.0 / D, bias=EPSV())
            recip(RSTD2(), RSTD2())
            for j in range(32):
                t6 = TMP6[j % 4]
                stt(t6(), X2T(j), A2(slice(j, j + 1)), RSTD2(), ALU.mult, ALU.mult)
                act(H2T(j), t6(), AF.Identity, bias=MODT(slice(M_SH2 + j, M_SH2 + j + 1)))

        def phase7(h):
            for cb in range(64):
                banks = fm_proj(wff1_d, cb * 256, 2, H2T)
                for jl in range(2):
                    j = cb * 2 + jl
                    rl = RL[jl]
                    act(rl(), banks[jl](), AF.Relu)
                    tt(AT(j), rl(), rl(), ALU.mult)

        def phase8(h):
            for cb in range(16):
                c0 = cb * 256
                pbs = [bank(acc_bank()) for _ in range(2)]
                for kb in range(8):
                    slot = ring_get(wff2_d, kb * 2048, c0)
                    for jl in range(2):
                        pairs = [(slot(k, slice(jl * 128, (jl + 1) * 128)).ap, AT(kb * 16 + k).ap) for k in range(16)]
                        mm_group(pbs[jl](), pairs, start=(kb == 0), stop=(kb == 7),
                                 reads=[slot(), AT(slice(kb * 16, kb * 16 + 16))])
                    ring_release()
                    if kb == 3:
                        flush_stats()
                rows = [h * 32 + cb * 2 + jl for jl in range(2)]
                srcs = [x2d[r * 128:(r + 1) * 128, :] for r in rows]
                for jl in range(2):
                    dma("sp", "xs%d" % jl, XS[jl](), dref(srcs[jl], DX2, rows[jl], rows[jl] + 1))
                for jl in range(2):
                    j = cb * 2 + jl
                    act(TT8[jl](), pbs[jl](), AF.Identity, scale=MODT(slice(M_GT2 + j, M_GT2 + j + 1)))
                for jl in range(2):
                    tt(X3S[jl](), XS[jl](), TT8[jl](), ALU.add)
                for jl in range(2):
                    j = cb * 2 + jl
                    sq = SQ8[(cb % 2) * 2 + jl]
                    act(sq(), X3S[jl](), AF.Square)
                    stats_mm(sq(), j)
                for jl in range(2):
                    dma("sp", "x3st%d" % jl, dref(srcs[jl], DX2, rows[jl], rows[jl] + 1), X3S[jl]())

        def phase9(h, out_ops):
            T0 = h * TP
            flush_stats()
            act(RSTD9(), bank(6)(), AF.Sqrt, scale=1.0 / D, bias=EPSV())
            recip(RSTD9(), RSTD9())
            for j in range(32):
                xs = XS9[j % 6]
                row = (h * 32 + j)
                dma("sp", "xs9%d" % (j % 6), xs(), dref(x2d[row * 128:(row + 1) * 128, :], DX2, row, row + 1))
                os_ = OS9[j % 4]
                stt(os_(), xs(), VECT(slice(C_GF + j, C_GF + j + 1)), RSTD9(), ALU.mult, ALU.mult)
                pb = bank(acc_bank(), F32, (4, 128))
                transposes([(pb(i).ap, os_(slice(i * 128, (i + 1) * 128)).ap, IDF().ap) for i in range(4)],
                           reads=[os_(), IDF()], writes=[pb()])
                if j % 2 == 0:
                    act(OT(slice(0, 4), slice(j * 128, (j + 1) * 128)), pb(), AF.Copy)
                else:
                    copy(OT(slice(0, 4), slice(j * 128, (j + 1) * 128)), pb())
                if j % 16 == 15:
                    hf = j // 16
                    for i in range(4):
                        dst = y_d[T0 + i * 128:T0 + (i + 1) * 128, hf * 2048:(hf + 1) * 2048]
                        row = (h * 4 + i) * 2 + hf
                        dma("sp", "out%d" % i, dref(dst, DY, row, row + 1), OT(i, slice(hf * 2048, (hf + 1) * 2048)))
                yield

        EPSV = pv(F32, (1,))

        dbgcol = [0]

        def dump(ref, ncols):
            if P.dry:
                return
            c0 = dbgcol[0]
            dbgcol[0] += ncols
            dma("pool", "dbg", dref(dbg_d[:, c0:c0 + ncols], DDBG, c0, c0 + ncols), ref)

        class _Stop(Exception):
            pass

        def chk(tag, items):
            if KSTOP == tag:
                for (r, n) in items:
                    dump(r, n)
                raise _Stop()

        def whole():
            try:
                whole_()
            except _Stop:
                pass

        def whole_():
            rot["acc"] = 0
            rot["half"] = 0
            ring["next_get"] = 0
            del pend_stats[:]
            if not P.dry:
                memset(EPSV(), EPS)
                setup()
            if KSTOP == "setup":
                dump(MODT(), 192); dump(A1(), 32); dump(A2(), 32); dump(WMT(0), 128); dump(WMT(15), 128); dump(VECT(), 224)
                return
            for h in range(2):
                if not P.dry and h == 0:
                    for _ in phase0(h):
                        pass
                if KSTOP == "p0":
                    dump(H1T(0), 512); dump(H1T(31), 512); dump(HALO(0), 16); dump(HALO(31), 16)
                    return
                phase1(h)
                if KSTOP == "p1":
                    dump(MIXB(0), 512); dump(MIXB(15), 512); dump(GVB(0, slice(0, 512)), 512)
                    return
                phase2(h)
                if KSTOP == "p2":
                    dump(MIXB(0), 512); dump(MIXB(15), 512)
                    return
                phase3(h)
                if KSTOP == "p3":
                    dump(YB(0), 512); dump(YB(15), 512); dump(POOLED(0), 512); dump(POOLED(15), 512)
                    return
                phase4(h)
                if KSTOP == "p4":
                    dump(MERGED(0), 512); dump(MERGED(31), 512)
                    return
                phase5(h)
                if not P.dry:
                    phase6(h)
                if KSTOP == "p6":
                    dump(X2T(0), 512); dump(X2T(31), 512); dump(H2T(0), 512); dump(H2T(31), 512)
                    return
                phase7(h)
                if KSTOP == "p7":
                    dump(AT(0), 512); dump(AT(127), 512)
                    return
                phase8(h)
                if not P.dry:
                    it9 = phase9(h, None)
                    it0 = phase0(h + 1) if h == 0 else iter(())
                    for step, _ in enumerate(it9):
                        if step % 6 == 5:
                            next(it0, None)
                    for _ in it0:
                        pass

        P.dry = True
        whole()
        P.dry = False
        whole()
        fence = P.add("sp", None, reads=[dref(None, DY, 0, 16), dref(None, DDBG, 0, 8192)])

        streams = P.streams()
        with contextlib.ExitStack() as es2:
            sems = {}
            for s in streams:
                sems[s] = es2.enter_context(nc.semaphore("s_" + s.replace(":", "_")))
            block = es2.enter_context(nc.Block())
            P.emit(nc, block, sems)
    return nc


def _tile(w, nk, ncols):
    kk, nn = w.shape
    nkb, ncb = kk // (nk * 128), nn // ncols
    t = w.reshape(nkb, nk, 128, ncb, ncols).transpose(0, 3, 2, 1, 4)
    return np.ascontiguousarray(t).reshape(nkb * ncb * 128, nk * ncols)


def _host_inputs(inputs):
    f = np.float32
    x = np.asarray(inputs["x"], f)[0]
    g = lambda k: np.asarray(inputs[k], f)
    vecs = np.concatenate([
        g("c")[0].reshape(32, 128), g("norm1_g")[0].reshape(32, 128), g("norm2_g")[0].reshape(32, 128),
        g("norm_f_g").reshape(32, 128), g("b_pool")[0].reshape(16, 128), g("pool_scale")[0].reshape(16, 128),
        g("b_gate")[0, 0].reshape(32, 128), g("b_gate")[0, 1].reshape(32, 128)], axis=0)
    common = {
        "vecs": np.ascontiguousarray(vecs),
        "bada": np.ascontiguousarray(g("b_ada")[0].reshape(192, 128)),
        "identf": np.eye(128, dtype=f),
        "tril": np.tril(np.ones((128, 128), f)),
        "lng": np.ascontiguousarray(np.broadcast_to(g("ln_v_g")[0][None, :], (128, 2048))),
        "lnb": np.ascontiguousarray(np.broadcast_to(g("ln_v_b")[0][None, :], (128, 2048))),
        "bsp": np.ascontiguousarray(np.broadcast_to(g("b_spatial")[0].reshape(1, 2048), (128, 2048))),
        "w_ada": g("w_ada")[0], "w_sp": g("w_spatial")[0],
        "w_pool": np.ascontiguousarray(g("w_pool")[0].reshape(2048, 512)),
        "wt_in": _tile(g("w_in")[0], 16, 256), "wt_v8": _tile(g("w_in")[0][:, 2048:4096], 8, 512),
        "wt_upa": _tile(g("w_up_a")[0], 16, 256), "wt_upb": _tile(g("w_up_b")[0], 16, 256),
        "wt_out": _tile(g("w_out")[0], 16, 256),
        "wt_ff1": _tile(g("w_ff1")[0], 16, 256), "wt_ff2": _tile(g("w_ff2")[0], 16, 256),
    }
    wins = np.array([2, 4, 8, 16], f)
    in_maps = []
    for c in range(NCORES):
        t0 = c * TOK
        m = dict(common)
        m["x"] = np.ascontiguousarray(x[t0:t0 + TOK])
        xh = np.zeros((32, D), f)
        hm = np.ones((128, 2), f)
        invc = np.zeros((2, 128, 4, TP), f)
        for h in range(2):
            s = t0 + h * TP
            if s >= 16:
                xh[h * 16:(h + 1) * 16] = x[s - 16:s]
            else:
                hm[:, h] = 0.0
            t = np.arange(s, s + TP, dtype=f)
            cnt = np.minimum(t[None, :] + 1.0, wins[:, None])
            invc[h] = (1.0 / cnt)[None]
        m["xh"] = xh
        m["hmask"] = hm
        m["invc"] = np.ascontiguousarray(invc.reshape(2 * 128, 4 * TP))
        in_maps.append(m)
    return in_maps


_NC_CACHE = {}


def kernel(**inputs):
    in_maps = _host_inputs(inputs)
    if "nc" not in _NC_CACHE:
        _NC_CACHE["nc"] = build_program()
    nc = _NC_CACHE["nc"]
    res = run_bass_kernel_spmd(nc, in_maps, core_ids=list(range(NCORES)))
    out = np.concatenate([np.asarray(r["y"], np.float32) for r in res.results], axis=0)
    return out.reshape(1, NCORES * TOK, D)
```

```python
import numpy as np
import concourse.bass as bass
import concourse.mybir as mybir
from concourse.bass_utils import run_bass_kernel_spmd

dt = mybir.dt
F32, BF16, F32R = dt.float32, dt.bfloat16, dt.float32r
AF = mybir.ActivationFunctionType
ALU = mybir.AluOpType
ES = {F32: 4, BF16: 2, F32R: 4}
K = 1024

NCORES = 8
TOK = 1024
TP = 512
D = 4096
DFF = 16384
NSLOT = 4
SLOT_B = 8 * K
EPS = 1e-6
GELU_C = 1.5957691216057308
USE_ACT_GELU = False
import os
KSTOP = os.environ.get("KSTOP", "")


class Arena:
    def __init__(self, name, t, base_dt):
        self.name, self.t, self.base_dt = name, t, base_dt


class Ref:
    __slots__ = ("ap", "arena", "lo", "hi")

    def __init__(self, ap, arena, lo, hi):
        self.ap, self.arena, self.lo, self.hi = ap, arena, lo, hi


class View:
    def __init__(self, arena, off, dtype, shape):
        self.arena, self.off, self.dtype, self.shape = arena, off, dtype, tuple(shape)
        es = ES[dtype]
        n = int(np.prod(shape))
        bes = ES[arena.base_dt]
        assert off % bes == 0 and (n * es) % bes == 0
        base = arena.t[:, off // bes:(off + n * es) // bes]
        ap = base.bitcast(dtype) if dtype != arena.base_dt else base
        if len(shape) == 2:
            ap = ap.rearrange("p (a b) -> p a b", b=shape[1])
        elif len(shape) == 3:
            ap = ap.rearrange("p (a b c) -> p a b c", b=shape[1], c=shape[2])
        self.full = ap
        st = []
        acc = 1
        for d_ in reversed(self.shape):
            st.append(acc)
            acc *= d_
        self.strides = tuple(reversed(st))

    def __call__(self, *idx, p=None):
        idx = tuple(idx) + (slice(None),) * (len(self.shape) - len(idx))
        lo = hi = 0
        for i, dim, st in zip(idx, self.shape, self.strides):
            if isinstance(i, int):
                assert 0 <= i < dim, (i, dim)
                lo += i * st
                hi += i * st
            else:
                a, b, _ = i.indices(dim)
                assert b > a
                lo += a * st
                hi += (b - 1) * st
        es = ES[self.dtype]
        psl = slice(None) if p is None else slice(p[0], p[1])
        ap = self.full[(psl,) + idx]
        return Ref(ap, self.arena, self.off + lo * es, self.off + (hi + 1) * es)


class DramArena:
    def __init__(self, name):
        self.name = name


def dref(ap, arena, lo=0, hi=1):
    return Ref(ap, arena, lo, hi)


class Op:
    __slots__ = ("eng", "stream", "fn", "deps", "needs_inc", "ticket", "is_dma", "idx")


class Rec:
    __slots__ = ("lo", "hi", "writer", "readers")


class Prog:
    ENGS = ("pe", "act", "dve", "pool", "sp")

    def __init__(self):
        self.ops = []
        self.recs = {}
        self.last_chan = {}
        self.dry = False

    def add(self, eng, fn, reads=(), writes=(), chan=None):
        if self.dry:
            return None
        op = Op()
        op.eng, op.fn, op.deps, op.needs_inc, op.ticket = eng, fn, {}, False, 0
        op.is_dma = chan is not None
        op.stream = ("dma:" + chan) if chan is not None else eng
        op.idx = len(self.ops)
        if chan is not None:
            prev = self.last_chan.get(chan)
            if prev is not None:
                op.deps[prev.idx] = prev
            self.last_chan[chan] = op
        for r in reads:
            self._access(op, r, False)
        for w in writes:
            self._access(op, w, True)
        self.ops.append(op)
        return op

    def _dep(self, op, other):
        if other is op:
            return
        if other.stream == "pe" and op.stream == "pe":
            return
        op.deps[other.idx] = other

    def _access(self, op, ref, is_write):
        name = ref.arena.name
        if name == "psum":
            ref = Ref(ref.ap, ref.arena, ref.lo // 2048 * 2048, (ref.hi + 2047) // 2048 * 2048)
            is_write = True
        recs = self.recs.get(name, [])
        new = []
        for r in recs:
            if r.hi <= ref.lo or r.lo >= ref.hi:
                new.append(r)
                continue
            if r.writer is not None:
                self._dep(op, r.writer)
            if is_write:
                for rd in r.readers.values():
                    self._dep(op, rd)
                if ref.lo <= r.lo and r.hi <= ref.hi:
                    continue
                new.append(r)
            else:
                r.readers[op.stream] = op
                new.append(r)
        if is_write:
            r = Rec()
            r.lo, r.hi, r.writer, r.readers = ref.lo, ref.hi, op, {}
            new.append(r)
        self.recs[name] = new

    def emit(self, nc, block, sems):
        for op in self.ops:
            for d_ in op.deps.values():
                d_.needs_inc = True
        cnt = {}
        for op in self.ops:
            if op.is_dma:
                cnt[op.stream] = cnt.get(op.stream, 0) + 16
                op.ticket = cnt[op.stream]
            elif op.needs_inc:
                cnt[op.stream] = cnt.get(op.stream, 0) + 1
                op.ticket = cnt[op.stream]
        self.max_counts = cnt

        def run(engname):
            def body(e):
                waited = {}
                for op in self.ops:
                    if op.eng != engname:
                        continue
                    for d_ in op.deps.values():
                        if waited.get(d_.stream, 0) < d_.ticket:
                            e.wait_ge(sems[d_.stream], d_.ticket)
                            waited[d_.stream] = d_.ticket
                    if op.fn is None:
                        continue
                    ins = op.fn(e)
                    if op.is_dma:
                        ins.then_inc(sems[op.stream], 16)
                    elif op.needs_inc:
                        ins.then_inc(sems[op.stream], 1)
            return body

        block.tensor(run("pe"))
        block.scalar(run("act"))
        block.vector(run("dve"))
        block.gpsimd(run("pool"))
        block.sync(run("sp"))

    def streams(self):
        s = []
        for op in self.ops:
            if op.stream not in s:
                s.append(op.stream)
        return s


def build_program():
    nc = bass.Bass("TRN2", target_bir_lowering=False)

    def din(name, shape):
        return nc.dram_tensor(name, list(shape), F32, kind="ExternalInput").ap()

    x_d = din("x", [TOK, D])
    xh_d = din("xh", [32, D])
    hmask_d = din("hmask", [128, 2])
    invc_d = din("invc", [2 * 128, 4 * TP])
    vecs_d = din("vecs", [224, 128])
    bada_d = din("bada", [192, 128])
    identf_d = din("identf", [128, 128])
    tril_d = din("tril", [128, 128])
    lng_d = din("lng", [128, 2048])
    lnb_d = din("lnb", [128, 2048])
    bsp_d = din("bsp", [128, 2048])
    wada_d = din("w_ada", [D, 6 * D])
    wt_in_d = din("wt_in", [2 * 56 * 128, 4096])
    wt_v8_d = din("wt_v8", [16 * 128, 4096])
    wsp_d = din("w_sp", [16, 128, 128])
    wpool_d = din("w_pool", [2048, 512])
    wt_upa_d = din("wt_upa", [16 * 128, 4096])
    wt_upb_d = din("wt_upb", [16 * 128, 4096])
    wt_out_d = din("wt_out", [32 * 128, 4096])
    wt_ff1_d = din("wt_ff1", [128 * 128, 4096])
    wt_ff2_d = din("wt_ff2", [128 * 128, 4096])
    TILED = {"in": (wt_in_d, 56), "v8": (wt_v8_d, 4), "upa": (wt_upa_d, 16), "upb": (wt_upb_d, 16),
             "out": (wt_out_d, 16), "ff1": (wt_ff1_d, 64), "ff2": (wt_ff2_d, 16)}
    win_d, wupa_d, wupb_d, wout_d, wff1_d, wff2_d = "in", "upa", "upb", "out", "ff1", "ff2"
    y_d = nc.dram_tensor("y", [TOK, D], F32, kind="ExternalOutput").ap()
    dbg_d = nc.dram_tensor("dbg", [128, 8192], F32, kind="ExternalOutput").ap() if KSTOP else None
    DDBG = DramArena("dbg")
    x2d = nc.dram_tensor("x2d", [2 * 32 * 128, TP], F32).ap()
    modd = nc.dram_tensor("modd", [192, 128], F32).ap()
    DX2 = DramArena("x2d")
    DMOD = DramArena("modd")
    DY = DramArena("y")
    DIN = DramArena("in")

    P = Prog()
    PERSIST_B = 10 * K
    SCR_B = 2 * K
    BIG_B = 160 * K

    import contextlib
    with contextlib.ExitStack() as es:
        persist_t = es.enter_context(nc.sbuf_tensor("persist", [128, PERSIST_B // 4], F32))
        scr_t = es.enter_context(nc.sbuf_tensor("scr", [128, SCR_B // 2], BF16))
        ring_t = es.enter_context(nc.sbuf_tensor("ring", [128, NSLOT * SLOT_B // 2], BF16))
        big_t = es.enter_context(nc.sbuf_tensor("big", [128, BIG_B // 2], BF16))
        psum_t = es.enter_context(nc.psum_tensor("ps", [128, 4096], F32))
        APER = Arena("persist", persist_t, F32)
        ASCR = Arena("scr", scr_t, BF16)
        ARING = Arena("ring", ring_t, BF16)
        ABIG = Arena("big", big_t, BF16)
        APS = Arena("psum", psum_t, F32)

        off = [0]

        def pv(dtype, shape):
            n = int(np.prod(shape)) * ES[dtype]
            n = (n + 31) // 32 * 32
            v = View(APER, off[0], dtype, shape)
            off[0] += n
            assert off[0] <= PERSIST_B
            return v

        IDF = pv(F32, (128,))
        IDB = pv(BF16, (128,))
        ONESB = pv(BF16, (128,))
        VECT = pv(F32, (224,))
        BADAT = pv(F32, (192,))
        MODT = pv(F32, (192,))
        A1 = pv(F32, (32,))
        A2 = pv(F32, (32,))
        CACT = pv(F32, (32,))
        WMT = pv(BF16, (16, 128))
        HMASK = pv(F32, (2,))
        SS = pv(F32, (1,))
        SS2 = pv(F32, (1,))
        RSTD1 = pv(F32, (1,))
        MV = pv(F32, (2,))
        VS = pv(F32, (1,))
        VR = pv(F32, (1,))
        NB = pv(F32, (1,))
        STATS = pv(F32, (4, 4, 6))
        PHSAVE = pv(F32, (16, 16))
        C_G1, C_G2, C_GF, C_BP, C_PS, C_BGA, C_BGB = 32, 64, 96, 128, 144, 160, 192
        M_SH1, M_SC1, M_GT1, M_SH2, M_SC2, M_GT2 = 0, 32, 64, 96, 128, 160

        RING = [View(ARING, s * SLOT_B, BF16, (16, 256)) for s in range(NSLOT)]
        RINGW = [View(ARING, s * SLOT_B, BF16, (8, 512)) for s in range(NSLOT)]
        RL = [View(ASCR, i * K, BF16, (TP,)) for i in range(2)]

        def bv(offk, dtype, shape):
            return View(ABIG, int(offk * K), dtype, shape)

        H1T = bv(0, BF16, (32, TP))
        HALO = bv(32, BF16, (32, 16))
        MIXB = bv(33, BF16, (16, TP))
        YB = bv(49, BF16, (16, TP))
        MERGED = bv(65, BF16, (32, TP))
        XT = [bv(97, F32, (D,)), bv(113, F32, (D,))]
        XN = bv(129, BF16, (D,))
        GVB = bv(97, BF16, (4, 2048))
        ZT = bv(113, F32, (2048,))
        LNG = bv(121, F32, (2048,))
        LNB = bv(129, F32, (2048,))
        BSP = bv(137, F32, (16, 128))
        GT = [bv(145, F32, (256,)), bv(146, F32, (256,)), bv(147, F32, (TP,)), bv(149, F32, (TP,))]
        WPOOL = bv(97, BF16, (16, 512))
        POOLED = bv(113, BF16, (16, TP))
        INVC = bv(129, F32, (4, TP))
        PT = [bv(137, F32, (528,)), bv(137 + 2.125, F32, (528,))]
        SA = bv(137 + 4.25, F32, (528,))
        SB = bv(137 + 6.375, F32, (528,))
        SGA = bv(97, F32, (2, TP))
        SGB = bv(101, F32, (2, TP))
        MP = bv(105, F32, (2, TP))
        X2T = bv(0, F32, (32, TP))
        XC = [bv(97, F32, (4, 256)), bv(101, F32, (4, 256))]
        TT5 = [bv(105, F32, (TP,)), bv(107, F32, (TP,))]
        SQ5 = [bv(109, BF16, (TP,)), bv(110, BF16, (TP,)), bv(117, BF16, (TP,)), bv(118, BF16, (TP,))]
        RSTD2 = bv(111, F32, (TP,))
        TMP6 = [bv(113, F32, (TP,)), bv(115, F32, (TP,)), bv(119, F32, (TP,)), bv(121, F32, (TP,))]
        H2T = bv(128, BF16, (32, TP))
        AT = bv(0, BF16, (128, TP))
        TT8 = [bv(128, F32, (TP,)), bv(130, F32, (TP,))]
        XS = [bv(132, F32, (TP,)), bv(134, F32, (TP,))]
        X3S = [bv(136, F32, (TP,)), bv(138, F32, (TP,))]
        SQ8 = [bv(140, BF16, (TP,)), bv(141, BF16, (TP,)), bv(144, BF16, (TP,)), bv(145, BF16, (TP,))]
        RSTD3 = bv(142, F32, (TP,))
        OT = bv(33, F32, (4, D))
        XS9 = [bv(137 + 2 * i, F32, (TP,)) for i in range(6)]
        OS9 = [bv(149 + 2 * i, F32, (TP,)) for i in range(4)]
        RSTD9 = bv(157, F32, (TP,))
        CBC = bv(0, F32, (32, 128))
        CBCB = bv(16, BF16, (32, 128))
        WA = [bv(52 + 16 * i, F32, (D,)) for i in range(3)]
        WAC = [bv(100 + 8 * i, BF16, (D,)) for i in range(3)]
        WAB = [bv(124 + 8 * i, BF16, (D,)) for i in range(3)]
        MODROW = bv(24, F32, (D,))
        WSP = bv(40, F32, (16, 128))
        RA = [bv(48 + 0.5 * i, F32, (128,)) for i in range(4)]
        TRIL = bv(50, F32, (128,))
        ONESF = bv(50.5, F32, (128,))

        def bank(b, dtype=F32, shape=(512,)):
            return View(APS, b * 2048, dtype, shape)

        def hbank(hb, dtype=F32, shape=(256,)):
            return View(APS, hb * 1024, dtype, shape)

        rot = {"acc": 0, "half": 0}

        def acc_bank():
            b = rot["acc"] % 6
            rot["acc"] += 1
            return b

        def acc_half():
            b = rot["half"] % 12
            rot["half"] += 1
            return b

        def dma(eng, chan, out, in_):
            P.add(eng, lambda e, o=out.ap, i=in_.ap: e.dma_start(out=o, in_=i),
                  reads=[in_], writes=[out], chan=chan)

        def act(out, in_, func, scale=1.0, bias=0.0, accum=None, extra_reads=()):
            def fn(e, o=out.ap, i=in_.ap):
                kw = {}
                if accum is not None:
                    kw["accum_out"] = accum.ap
                sc = scale.ap if isinstance(scale, Ref) else float(scale)
                bi = bias.ap if isinstance(bias, Ref) else float(bias)
                return e.activation(o, i, func, bias=bi, scale=sc, **kw)
            rd = [in_] + list(extra_reads)
            if isinstance(scale, Ref):
                rd.append(scale)
            if isinstance(bias, Ref):
                rd.append(bias)
            wr = [out] + ([accum] if accum is not None else [])
            P.add("act", fn, reads=rd, writes=wr)

        def ts(out, in0, s1, s2, op0, op1=None, eng="dve"):
            def fn(e, o=out.ap, i=in0.ap):
                a = s1.ap if isinstance(s1, Ref) else s1
                b = s2.ap if isinstance(s2, Ref) else s2
                if op1 is None:
                    return e.tensor_scalar(o, i, a, None, op0)
                return e.tensor_scalar(o, i, a, b, op0, op1)
            rd = [in0] + [s for s in (s1, s2) if isinstance(s, Ref)]
            P.add(eng, fn, reads=rd, writes=[out])

        def tt(out, in0, in1, op, eng="dve"):
            P.add(eng, lambda e, o=out.ap, a=in0.ap, b=in1.ap: e.tensor_tensor(o, a, b, op),
                  reads=[in0, in1], writes=[out])

        def stt(out, in0, scalar, in1, op0, op1):
            def fn(e, o=out.ap, a=in0.ap, b=in1.ap):
                s = scalar.ap if isinstance(scalar, Ref) else scalar
                return e.scalar_tensor_tensor(o, a, s, b, op0, op1)
            rd = [in0, in1] + ([scalar] if isinstance(scalar, Ref) else [])
            P.add("dve", fn, reads=rd, writes=[out])

        def recip(out, in_):
            P.add("dve", lambda e, o=out.ap, i=in_.ap: e.reciprocal(o, i), reads=[in_], writes=[out])

        def copy(out, in_, eng="dve"):
            P.add(eng, lambda e, o=out.ap, i=in_.ap: e.tensor_copy(o, i), reads=[in_], writes=[out])

        def memset(out, val, eng="dve"):
            P.add(eng, lambda e, o=out.ap: e.memset(o, val), writes=[out])

        def mm_group(out, pairs, start, stop, reads):
            n = len(pairs)

            def fn(e, o=out.ap):
                ins = None
                for q, (l, r) in enumerate(pairs):
                    ins = e.matmul(o, l, r, start=(start and q == 0), stop=(stop and q == n - 1))
                return ins
            P.add("pe", fn, reads=reads, writes=[out])

        def transposes(items, reads, writes):
            def fn(e):
                ins = None
                for (o, i, idn) in items:
                    ins = e.transpose(o, i, idn)
                return ins
            P.add("pe", fn, reads=reads, writes=writes)

        def gelu_evac(src, dst, tA):
            if USE_ACT_GELU:
                act(dst, src, AF.Gelu_apprx_tanh)
                return
            act(tA, src, AF.Square)
            ts(tA, tA, 0.044715, 1.0, ALU.mult, ALU.add)
            tt(tA, tA, src, ALU.mult)
            act(tA, tA, AF.Sigmoid, scale=GELU_C)
            tt(dst, tA, src, ALU.mult)

        ring = {"specs": [], "next_get": 0, "next_load": 0}

        def ring_load(n):
            wd, r0, c0, nk, ncols = ring["specs"][n]
            s = n % NSLOT
            if ncols == 512:
                tap, ncb = TILED["v8"]
                blk = (r0 // 1024) * ncb + (c0 - 2048) // 512
            else:
                tap, ncb = TILED[wd]
                blk = (r0 // 2048) * ncb + c0 // 256
            src = tap[blk * 128:(blk + 1) * 128, 0:nk * ncols].rearrange("p (k c) -> p k c", c=ncols)
            dst = RING[s] if ncols == 256 else RINGW[s]
            dma("pool", "ring%d" % s, dst(slice(0, nk)), dref(src, DIN))

        def ring_get(wd, r0, c0, nk=16, ncols=256):
            n = ring["next_get"]
            ring["next_get"] += 1
            ret = RING[n % NSLOT] if ncols == 256 else RINGW[n % NSLOT]
            if P.dry:
                ring["specs"].append((wd, r0, c0, nk, ncols))
                return ret
            if n == 0:
                for m in range(min(NSLOT, len(ring["specs"]))):
                    ring_load(m)
                ring["next_load"] = NSLOT
            assert ring["specs"][n][1:] == (r0, c0, nk, ncols)
            return ret

        def ring_release():
            if P.dry:
                return
            m = ring["next_load"]
            if m < len(ring["specs"]):
                ring_load(m)
            ring["next_load"] += 1

        def load_rows_T(src_d, nrows, dstT, col0, arena_d=DIN):
            done = 0
            q = 0
            while done < nrows:
                n = min(128, nrows - done)
                ra = RA[q % 4]
                dma("sp", "ra%d" % (q % 4), ra(p=(0, n)), dref(src_d[done:done + n, :], arena_d))
                b = 6 + (q % 2)
                pb = bank(b, F32, (128,))
                transposes([(pb(slice(0, n)).ap, ra(p=(0, n)).ap, IDF(slice(0, n), p=(0, n)).ap)],
                           reads=[ra(p=(0, n)), IDF()], writes=[pb(slice(0, n))])
                copy(dstT(slice(col0 + done, col0 + done + n)), pb(slice(0, n)))
                done += n
                q += 1

        def setup():
            dma("sp", "c0", IDF(), dref(identf_d, DIN))
            dma("sp", "c1", TRIL(), dref(tril_d, DIN))
            dma("sp", "c2", HMASK(), dref(hmask_d, DIN))
            memset(ONESF(), 1.0)
            memset(ONESB(), 1.0)
            copy(IDB(), IDF())
            load_rows_T(vecs_d, 224, VECT, 0)
            load_rows_T(bada_d, 192, BADAT, 0)
            act(CACT(), VECT(slice(0, 32)), AF.Sigmoid)
            tt(CACT(), CACT(), VECT(slice(0, 32)), ALU.mult)
            for kc in range(32):
                ts(CBC(kc), ONESF(), CACT(slice(kc, kc + 1)), None, ALU.mult)
            copy(CBCB(), CBC())
            qf = 0
            qb = 0
            for g in range(6):
                for kc in range(32):
                    src = dref(wada_d[kc * 128:(kc + 1) * 128, g * D:(g + 1) * D], DIN)
                    if kc % 2 == 0:
                        wf = WA[qf % 3]
                        dma("sp", "wa%d" % (qf % 3), wf(), src)
                        wa = WAC[qf % 3]
                        if qf % 2 == 0:
                            act(wa(), wf(), AF.Copy)
                        else:
                            copy(wa(), wf())
                        qf += 1
                        lhs = CBCB(kc)
                    else:
                        wa = WAB[qb % 3]
                        dma("pool", "wab%d" % (qb % 3), wa(), src)
                        qb += 1
                        lhs = CBCB(kc)
                    for b in range(8):
                        pb = bank(b)
                        mm_group(pb(), [(lhs.ap, wa(slice(b * 512, (b + 1) * 512)).ap)],
                                 start=(kc == 0), stop=(kc == 31), reads=[lhs, wa()])
                for b in range(8):
                    act(MODROW(slice(b * 512, (b + 1) * 512), p=(0, 1)), bank(b)(p=(0, 1)), AF.Copy)
                dst = modd[g * 32:(g + 1) * 32, :].rearrange("(o r) c -> o (r c)", o=1)
                dma("sp", "modst", dref(dst, DMOD, g, g + 1), MODROW(p=(0, 1)))
            done = 0
            for q2, n in enumerate((128, 64)):
                ra = RA[q2]
                dma("sp", "ra%d" % q2, ra(p=(0, n)), dref(modd[done:done + n, :], DMOD, 0, 6))
                pb = bank(6 + q2, F32, (128,))
                transposes([(pb(slice(0, n)).ap, ra(p=(0, n)).ap, IDF(slice(0, n), p=(0, n)).ap)],
                           reads=[ra(p=(0, n)), IDF()], writes=[pb(slice(0, n))])
                tt(MODT(slice(done, done + n)), pb(slice(0, n)), BADAT(slice(done, done + n)), ALU.add)
                done += n
            stt(A1(), MODT(slice(M_SC1, M_SC1 + 32)), 1.0, VECT(slice(C_G1, C_G1 + 32)), ALU.add, ALU.mult)
            stt(A2(), MODT(slice(M_SC2, M_SC2 + 32)), 1.0, VECT(slice(C_G2, C_G2 + 32)), ALU.add, ALU.mult)
            dma("sp", "c3", WSP(), dref(wsp_d.rearrange("h t s -> t h s"), DIN))
            for hd in range(16):
                tt(WSP(hd), WSP(hd), TRIL(), ALU.mult)
            for g in range(4):
                pb = bank(6 + (g % 2), F32, (4, 128))
                transposes([(pb(q).ap, WSP(g * 4 + q).ap, IDF().ap) for q in range(4)],
                           reads=[WSP(slice(g * 4, g * 4 + 4)), IDF()], writes=[pb()])
                copy(WMT(slice(g * 4, g * 4 + 4)), pb())

        def phase0(h):
            T0 = h * TP
            for i in range(5 if h == 0 else 4):
                halo = (i == 4)
                n = 16 if halo else 128
                pp = (0, n)
                xt = XT[i % 2]
                src = xh_d[h * 16:(h + 1) * 16, :] if halo else x_d[T0 + i * 128:T0 + (i + 1) * 128, :]
                dma("sp", "xt%d" % (i % 2), xt(p=pp), dref(src, DIN))
                act(XN(p=pp), xt(p=pp), AF.Square, accum=SS(p=pp))
                act(SS2(p=pp), SS(p=pp), AF.Sqrt, scale=1.0 / D, bias=EPSV(p=pp))
                recip(RSTD1(p=pp), SS2(p=pp))
                act(XN(p=pp), xt(p=pp), AF.Identity, scale=RSTD1(p=pp))
                for g in range(4):
                    pb = bank(6 + (g % 2), BF16, (8, 128))
                    items = []
                    for q in range(8):
                        kc = g * 8 + q
                        items.append((pb(q, slice(0, n)).ap, XN(slice(kc * 128, (kc + 1) * 128), p=pp).ap,
                                      IDB(slice(0, n), p=pp).ap))
                    transposes(items, reads=[XN(p=pp), IDB()], writes=[pb()])
                    for q in range(8):
                        kc = g * 8 + q
                        dst = HALO(kc) if halo else H1T(kc, slice(i * 128, (i + 1) * 128))
                        ts(dst, pb(q, slice(0, n)), A1(slice(kc, kc + 1)), MODT(slice(M_SH1 + kc, M_SH1 + kc + 1)),
                           ALU.mult, ALU.add)
                yield

        def phase1(h):
            dma("sp", "lng", LNG(), dref(lng_d, DIN))
            dma("sp", "lnb", LNB(), dref(lnb_d, DIN))
            dma("sp", "bsp", BSP(), dref(bsp_d.rearrange("p (a b) -> p a b", b=128), DIN))
            for cb in range(4):
                c0 = 2048 + cb * 512
                pbs = [bank(acc_bank()) for _ in range(4)]
                for kb in range(4):
                    slot = ring_get(win_d, kb * 1024, c0, 8, 512)
                    for i in range(4):
                        pairs = [(H1T(kb * 8 + k, slice(i * 128, (i + 1) * 128)).ap, slot(k).ap) for k in range(8)]
                        mm_group(pbs[i](), pairs, start=(kb == 0), stop=(kb == 3),
                                 reads=[slot(), H1T(slice(kb * 8, kb * 8 + 8))])
                    ring_release()
                for i in range(4):
                    gdst = GVB(i, slice(cb * 512, (cb + 1) * 512))
                    gelu_evac(pbs[i](), gdst, GT[2 + (i % 2)]())
                    P.add("dve", lambda e, o=STATS(i, cb).ap, a=gdst.ap: e.bn_stats(o, a),
                          reads=[gdst], writes=[STATS(i, cb)])
            chk("p1a", [(GVB(0, slice(0, 512)), 512), (STATS(0), 24)])
            for i in range(4):
                P.add("dve", lambda e, o=MV().ap, a=STATS(i).ap: e.bn_aggr(o, a), reads=[STATS(i)], writes=[MV()])
                act(VS(), MV(slice(1, 2)), AF.Sqrt, scale=1.0, bias=EPSV())
                recip(VR(), VS())
                stt(NB(), MV(slice(0, 1)), -1.0, VR(), ALU.mult, ALU.mult)
                act(ZT(), GVB(i), AF.Identity, scale=VR(), bias=NB())
                tt(ZT(), ZT(), LNG(), ALU.mult)
                tt(GVB(i), ZT(), LNB(), ALU.add)
                if i == 0:
                    chk("p1b", [(GVB(0, slice(0, 512)), 512), (MV(), 2)])
                for g in range(4):
                    pb = bank(6 + (g % 2), F32, (4, 128))
                    for q in range(4):
                        hd = g * 4 + q
                        mm_group(pb(q), [(GVB(i, slice(hd * 128, (hd + 1) * 128)).ap, WMT(hd).ap)], True, True,
                                 reads=[GVB(i), WMT(hd)])
                    tt(MIXB(slice(g * 4, g * 4 + 4), slice(i * 128, (i + 1) * 128)), pb(),
                       BSP(slice(g * 4, g * 4 + 4)), ALU.add)

        def fm_proj(wd, c0, nkb, act_view, nchunks_rows=16):
            banks = [bank(acc_bank()) for _ in range(2)]
            for kb in range(nkb):
                slot = ring_get(wd, kb * 2048, c0, nchunks_rows)
                for jl in range(2):
                    pairs = [(slot(k, slice(jl * 128, (jl + 1) * 128)).ap, act_view(kb * 16 + k).ap)
                             for k in range(nchunks_rows)]
                    mm_group(banks[jl](), pairs, start=(kb == 0), stop=(kb == nkb - 1),
                             reads=[slot(), act_view(slice(kb * 16, kb * 16 + nchunks_rows))])
                ring_release()
            return banks

        def phase2(h):
            for cb in range(8):
                banks = fm_proj(win_d, cb * 256, 2, H1T)
                for jl in range(2):
                    j = cb * 2 + jl
                    tmp = GT[2 + jl]
                    t2 = TT5[jl]
                    gelu_evac(banks[jl](), t2(), tmp())
                    tt(MIXB(j), t2(), MIXB(j), ALU.mult)

        def phase3(h):
            dma("pool", "wpool", WPOOL(), dref(wpool_d.rearrange("(a p) d -> p a d", p=128), DIN))
            dma("sp", "invc", INVC(), dref(invc_d[h * 128:(h + 1) * 128, :].rearrange("p (a b) -> p a b", b=TP), DIN))
            for cb in range(8):
                c0 = 4096 + cb * 256
                pbs = [bank(acc_bank()) for _ in range(2)]
                hbs = [bank(6 + jl, F32, (16,)) for jl in range(2)]
                for kb in range(2):
                    slot = ring_get(win_d, kb * 2048, c0)
                    for jl in range(2):
                        pairs = [(slot(k, slice(jl * 128, (jl + 1) * 128)).ap, H1T(kb * 16 + k).ap) for k in range(16)]
                        mm_group(pbs[jl](), pairs, start=(kb == 0), stop=(kb == 1),
                                 reads=[slot(), H1T(slice(kb * 16, kb * 16 + 16))])
                        if h == 0:
                            pairs = [(slot(k, slice(jl * 128, (jl + 1) * 128)).ap, HALO(kb * 16 + k).ap) for k in range(16)]
                            mm_group(hbs[jl](), pairs, start=(kb == 0), stop=(kb == 1),
                                     reads=[slot(), HALO(slice(kb * 16, kb * 16 + 16))])
                    ring_release()
                for jl in range(2):
                    j = cb * 2 + jl
                    g = j // 4
                    pb = pbs[jl]
                    hb = hbs[jl]
                    pt = PT[j % 2]
                    act(pt(slice(16, 528)), pb(), AF.Copy)
                    if h == 0:
                        ts(pt(slice(0, 16)), hb(), HMASK(slice(h, h + 1)), None, ALU.mult)
                        copy(PHSAVE(j), pt(slice(512, 528)))
                    else:
                        copy(pt(slice(0, 16)), PHSAVE(j))
                    tt(SA(slice(1, 528)), pt(slice(1, 528)), pt(slice(0, 527)), ALU.add)
                    cur = SA
                    if g >= 1:
                        tt(SB(slice(3, 528)), SA(slice(3, 528)), SA(slice(1, 526)), ALU.add)
                        cur = SB
                    if g >= 2:
                        tt(SA(slice(7, 528)), SB(slice(7, 528)), SB(slice(3, 524)), ALU.add)
                        cur = SA
                    if g >= 3:
                        tt(SB(slice(15, 528)), SA(slice(15, 528)), SA(slice(7, 520)), ALU.add)
                        cur = SB
                    oth = SB if cur is SA else SA
                    tt(oth(slice(16, 528)), cur(slice(16, 528)), INVC(g), ALU.mult)
                    tt(POOLED(j), oth(slice(16, 528)), pt(slice(16, 528)), ALU.subtract)
            for g in range(4):
                for dj in range(4):
                    pb = bank(acc_bank())
                    pairs = [(WPOOL(g * 4 + cc, slice(dj * 128, (dj + 1) * 128)).ap, POOLED(g * 4 + cc).ap)
                             for cc in range(4)]
                    mm_group(pb(), pairs, True, True, reads=[WPOOL(), POOLED(slice(g * 4, g * 4 + 4))])
                    j = g * 4 + dj
                    ts(YB(j), pb(), VECT(slice(C_BP + j, C_BP + j + 1)), VECT(slice(C_PS + j, C_PS + j + 1)),
                       ALU.add, ALU.mult)

        def phase4(h):
            for jp in range(16):
                banks = fm_proj(win_d, 6144 + jp * 256, 2, H1T)
                for jl in range(2):
                    j = jp * 2 + jl
                    act(SGA(jl), banks[jl](), AF.Sigmoid, bias=VECT(slice(C_BGA + j, C_BGA + j + 1)))
                banks = fm_proj(win_d, 10240 + jp * 256, 2, H1T)
                for jl in range(2):
                    j = jp * 2 + jl
                    act(SGB(jl), banks[jl](), AF.Sigmoid, bias=VECT(slice(C_BGB + j, C_BGB + j + 1)))
                banks = fm_proj(wupa_d, jp * 256, 1, MIXB)
                for jl in range(2):
                    tt(MP(jl), banks[jl](), SGA(jl), ALU.mult)
                banks = fm_proj(wupb_d, jp * 256, 1, YB)
                for jl in range(2):
                    j = jp * 2 + jl
                    tt(SGB(jl), banks[jl](), SGB(jl), ALU.mult)
                    tt(MERGED(j), MP(jl), SGB(jl), ALU.add)

        pend_stats = []

        def stats_mm(sq, j):
            pend_stats.append((sq, j))

        def flush_stats():
            for (sq, j) in pend_stats:
                mm_group(bank(6)(), [(ONESB().ap, sq.ap)], start=(j == 0), stop=(j == 31), reads=[ONESB(), sq])
            del pend_stats[:]

        def phase5(h):
            T0 = h * TP

            def load_xc(jp):
                src = x_d[T0:T0 + TP, jp * 256:(jp + 1) * 256].rearrange("(i p) c -> p i c", p=128)
                dma("sp", "xc%d" % (jp % 2), XC[jp % 2](), dref(src, DIN))

            load_xc(0)
            for jp in range(16):
                xc = XC[jp % 2]
                xbs = []
                for jl in range(2):
                    xb = bank(acc_bank(), F32, (4, 128))
                    transposes([(xb(i).ap, xc(i, slice(jl * 128, (jl + 1) * 128)).ap, IDF().ap) for i in range(4)],
                               reads=[xc(), IDF()], writes=[xb()])
                    xbs.append(xb)
                if jp + 1 < 16:
                    load_xc(jp + 1)
                banks = fm_proj(wout_d, jp * 256, 2, MERGED)
                flush_stats()
                for jl in range(2):
                    j = jp * 2 + jl
                    act(TT5[jl](), banks[jl](), AF.Identity, scale=MODT(slice(M_GT1 + j, M_GT1 + j + 1)))
                for jl in range(2):
                    j = jp * 2 + jl
                    tt(X2T(j), xbs[jl](), TT5[jl](), ALU.add)
                for jl in range(2):
                    j = jp * 2 + jl
                    sq = SQ5[(jp % 2) * 2 + jl]
                    act(sq(), X2T(j), AF.Square)
                    stats_mm(sq(), j)
                for jl in range(2):
                    j = jp * 2 + jl
                    dst = x2d[(h * 32 + j) * 128:(h * 32 + j + 1) * 128, :]
                    dma("sp", "x2st%d" % (j % 2), dref(dst, DX2, h * 32 + j, h * 32 + j + 1), X2T(j))

        def phase6(h):
            flush_stats()
            act(RSTD2(), bank(6)(), AF.Sqrt, scale=1.0 / D, bias=EPSV())
            recip(RSTD2(), RSTD2())
            for j in range(32):
                t6 = TMP6[j % 4]
                stt(t6(), X2T(j), A2(slice(j, j + 1)), RSTD2(), ALU.mult, ALU.mult)
                act(H2T(j), t6(), AF.Identity, bias=MODT(slice(M_SH2 + j, M_SH2 + j + 1)))

        def phase7(h):
            for cb in range(64):
                banks = fm_proj(wff1_d, cb * 256, 2, H2T)
                for jl in range(2):
                    j = cb * 2 + jl
                    rl = RL[jl]
                    act(rl(), banks[jl](), AF.Relu)
                    tt(AT(j), rl(), rl(), ALU.mult)

        def phase8(h):
            for cb in range(16):
                c0 = cb * 256
                pbs = [bank(acc_bank()) for _ in range(2)]
                for kb in range(8):
                    slot = ring_get(wff2_d, kb * 2048, c0)
                    for jl in range(2):
                        pairs = [(slot(k, slice(jl * 128, (jl + 1) * 128)).ap, AT(kb * 16 + k).ap) for k in range(16)]
                        mm_group(pbs[jl](), pairs, start=(kb == 0), stop=(kb == 7),
                                 reads=[slot(), AT(slice(kb * 16, kb * 16 + 16))])
                    ring_release()
                    if kb == 3:
                        flush_stats()
                rows = [h * 32 + cb * 2 + jl for jl in range(2)]
                srcs = [x2d[r * 128:(r + 1) * 128, :] for r in rows]
                for jl in range(2):
                    dma("sp", "xs%d" % jl, XS[jl](), dref(srcs[jl], DX2, rows[jl], rows[jl] + 1))
                for jl in range(2):
                    j = cb * 2 + jl
                    act(TT8[jl](), pbs[jl](), AF.Identity, scale=MODT(slice(M_GT2 + j, M_GT2 + j + 1)))
                for jl in range(2):
                    tt(X3S[jl](), XS[jl](), TT8[jl](), ALU.add)
                for jl in range(2):
                    j = cb * 2 + jl
                    sq = SQ8[(cb % 2) * 2 + jl]
                    act(sq(), X3S[jl](), AF.Square)
                    stats_mm(sq(), j)
                for jl in range(2):
                    dma("sp", "x3st%d" % jl, dref(srcs[jl], DX2, rows[jl], rows[jl] + 1), X3S[jl]())

        def phase9(h, out_ops):
            T0 = h * TP
            flush_stats()
            act(RSTD9(), bank(6)(), AF.Sqrt, scale=1.0 / D, bias=EPSV())
            recip(RSTD9(), RSTD9())
            for j in range(32):
                xs = XS9[j % 6]
                row = (h * 32 + j)
                dma("sp", "xs9%d" % (j % 6), xs(), dref(x2d[row * 128:(row + 1) * 128, :], DX2, row, row + 1))
                os_ = OS9[j % 4]
                stt(os_(), xs(), VECT(slice(C_GF + j, C_GF + j + 1)), RSTD9(), ALU.mult, ALU.mult)
                pb = bank(acc_bank(), F32, (4, 128))
                transposes([(pb(i).ap, os_(slice(i * 128, (i + 1) * 128)).ap, IDF().ap) for i in range(4)],
                           reads=[os_(), IDF()], writes=[pb()])
                if j % 2 == 0:
                    act(OT(slice(0, 4), slice(j * 128, (j + 1) * 128)), pb(), AF.Copy)
                else:
                    copy(OT(slice(0, 4), slice(j * 128, (j + 1) * 128)), pb())
                if j % 16 == 15:
                    hf = j // 16
                    for i in range(4):
                        dst = y_d[T0 + i * 128:T0 + (i + 1) * 128, hf * 2048:(hf + 1) * 2048]
                        row = (h * 4 + i) * 2 + hf
                        dma("sp", "out%d" % i, dref(dst, DY, row, row + 1), OT(i, slice(hf * 2048, (hf + 1) * 2048)))
                yield

        EPSV = pv(F32, (1,))

        dbgcol = [0]

        def dump(ref, ncols):
            if P.dry:
                return
            c0 = dbgcol[0]
            dbgcol[0] += ncols
            dma("pool", "dbg", dref(dbg_d[:, c0:c0 + ncols], DDBG, c0, c0 + ncols), ref)

        class _Stop(Exception):
            pass

        def chk(tag, items):
            if KSTOP == tag:
                for (r, n) in items:
                    dump(r, n)
                raise _Stop()

        def whole():
            try:
                whole_()
            except _Stop:
                pass

        def whole_():
            rot["acc"] = 0
            rot["half"] = 0
            ring["next_get"] = 0
            del pend_stats[:]
            if not P.dry:
                memset(EPSV(), EPS)
                setup()
            if KSTOP == "setup":
                dump(MODT(), 192); dump(A1(), 32); dump(A2(), 32); dump(WMT(0), 128); dump(WMT(15), 128); dump(VECT(), 224)
                return
            for h in range(2):
                if not P.dry and h == 0:
                    for _ in phase0(h):
                        pass
                if KSTOP == "p0":
                    dump(H1T(0), 512); dump(H1T(31), 512); dump(HALO(0), 16); dump(HALO(31), 16)
                    return
                phase1(h)
                if KSTOP == "p1":
                    dump(MIXB(0), 512); dump(MIXB(15), 512); dump(GVB(0, slice(0, 512)), 512)
                    return
                phase2(h)
                if KSTOP == "p2":
                    dump(MIXB(0), 512); dump(MIXB(15), 512)
                    return
                phase3(h)
                if KSTOP == "p3":
                    dump(YB(0), 512); dump(YB(15), 512); dump(POOLED(0), 512); dump(POOLED(15), 512)
                    return
                phase4(h)
                if KSTOP == "p4":
                    dump(MERGED(0), 512); dump(MERGED(31), 512)
                    return
                phase5(h)
                if not P.dry:
                    phase6(h)
                if KSTOP == "p6":
                    dump(X2T(0), 512); dump(X2T(31), 512); dump(H2T(0), 512); dump(H2T(31), 512)
                    return
                phase7(h)
                if KSTOP == "p7":
                    dump(AT(0), 512); dump(AT(127), 512)
                    return
                phase8(h)
                if not P.dry:
                    it9 = phase9(h, None)
                    it0 = phase0(h + 1) if h == 0 else iter(())
                    for step, _ in enumerate(it9):
                        if step % 6 == 5:
                            next(it0, None)
                    for _ in it0:
                        pass

        P.dry = True
        whole()
        P.dry = False
        whole()
        fence = P.add("sp", None, reads=[dref(None, DY, 0, 16), dref(None, DDBG, 0, 8192)])

        streams = P.streams()
        with contextlib.ExitStack() as es2:
            sems = {}
            for s in streams:
                sems[s] = es2.enter_context(nc.semaphore("s_" + s.replace(":", "_")))
            block = es2.enter_context(nc.Block())
            P.emit(nc, block, sems)
    return nc


def _tile(w, nk, ncols):
    kk, nn = w.shape
    nkb, ncb = kk // (nk * 128), nn // ncols
    t = w.reshape(nkb, nk, 128, ncb, ncols).transpose(0, 3, 2, 1, 4)
    return np.ascontiguousarray(t).reshape(nkb * ncb * 128, nk * ncols)


def _host_inputs(inputs):
    f = np.float32
    x = np.asarray(inputs["x"], f)[0]
    g = lambda k: np.asarray(inputs[k], f)
    vecs = np.concatenate([
        g("c")[0].reshape(32, 128), g("norm1_g")[0].reshape(32, 128), g("norm2_g")[0].reshape(32, 128),
        g("norm_f_g").reshape(32, 128), g("b_pool")[0].reshape(16, 128), g("pool_scale")[0].reshape(16, 128),
        g("b_gate")[0, 0].reshape(32, 128), g("b_gate")[0, 1].reshape(32, 128)], axis=0)
    common = {
        "vecs": np.ascontiguousarray(vecs),
        "bada": np.ascontiguousarray(g("b_ada")[0].reshape(192, 128)),
        "identf": np.eye(128, dtype=f),
        "tril": np.tril(np.ones((128, 128), f)),
        "lng": np.ascontiguousarray(np.broadcast_to(g("ln_v_g")[0][None, :], (128, 2048))),
        "lnb": np.ascontiguousarray(np.broadcast_to(g("ln_v_b")[0][None, :], (128, 2048))),
        "bsp": np.ascontiguousarray(np.broadcast_to(g("b_spatial")[0].reshape(1, 2048), (128, 2048))),
        "w_ada": g("w_ada")[0], "w_sp": g("w_spatial")[0],
        "w_pool": np.ascontiguousarray(g("w_pool")[0].reshape(2048, 512)),
        "wt_in": _tile(g("w_in")[0], 16, 256), "wt_v8": _tile(g("w_in")[0][:, 2048:4096], 8, 512),
        "wt_upa": _tile(g("w_up_a")[0], 16, 256), "wt_upb": _tile(g("w_up_b")[0], 16, 256),
        "wt_out": _tile(g("w_out")[0], 16, 256),
        "wt_ff1": _tile(g("w_ff1")[0], 16, 256), "wt_ff2": _tile(g("w_ff2")[0], 16, 256),
    }
    wins = np.array([2, 4, 8, 16], f)
    in_maps = []
    for c in range(NCORES):
        t0 = c * TOK
        m = dict(common)
        m["x"] = np.ascontiguousarray(x[t0:t0 + TOK])
        xh = np.zeros((32, D), f)
        hm = np.ones((128, 2), f)
        invc = np.zeros((2, 128, 4, TP), f)
        for h in range(2):
            s = t0 + h * TP
            if s >= 16:
                xh[h * 16:(h + 1) * 16] = x[s - 16:s]
            else:
                hm[:, h] = 0.0
            t = np.arange(s, s + TP, dtype=f)
            cnt = np.minimum(t[None, :] + 1.0, wins[:, None])
            invc[h] = (1.0 / cnt)[None]
        m["xh"] = xh
        m["hmask"] = hm
        m["invc"] = np.ascontiguousarray(invc.reshape(2 * 128, 4 * TP))
        in_maps.append(m)
    return in_maps


_NC_CACHE = {}


def kernel(**inputs):
    in_maps = _host_inputs(inputs)
    if "nc" not in _NC_CACHE:
        _NC_CACHE["nc"] = build_program()
    nc = _NC_CACHE["nc"]
    res = run_bass_kernel_spmd(nc, in_maps, core_ids=list(range(NCORES)))
    out = np.concatenate([np.asarray(r["y"], np.float32) for r in res.results], axis=0)
    return out.reshape(1, NCORES * TOK, D)
```
